# Optimizing a Trainium2 kernel written in Bass

```python
import math
import jax, jax.numpy as jnp
from jax import lax
import numpy as np

D_MODEL = 1024
BATCH = 16
SEQ = 4096
DEPTH = 4

N_MIXERS = 3
CONV_WIDTH = 31
POOL_WINDOWS = (2, 4, 8, 16)
N_POOL_GROUPS = len(POOL_WINDOWS)
POOL_GROUP_DIM = D_MODEL // N_POOL_GROUPS
N_HEADS = 16
HEAD_DIM = D_MODEL // N_HEADS
Q_BLOCK = 128
D_FF = ((8 * D_MODEL // 3 + 127) // 128) * 128
FFN_CONV_WIDTH = 3
EPS = 1e-6
N_A = (DEPTH + 2) // 3
N_B = (DEPTH + 1) // 3
N_C = DEPTH // 3

kernel_name = "hybrid_conv_pool_fox_trunk"


def rms_norm(x, g):
    x32 = x.astype(jnp.float32)
    y = x32 * lax.rsqrt(jnp.mean(x32 * x32, axis=-1, keepdims=True) + EPS)
    return (y * g.astype(jnp.float32)).astype(x.dtype)


def causal_dwconv(x, w, b):
    k_width, c = w.shape
    y = lax.conv_general_dilated(x, w[:, None, :].astype(x.dtype), window_strides=(1,),
                                 padding=[(k_width - 1, 0)],
                                 dimension_numbers=("NWC", "WIO", "NWC"),
                                 feature_group_count=c)
    return y + b.astype(x.dtype)


def conformer_conv(h, w_in, b_in, dw, dw_b, ln_g, ln_b, w_out, b_out):
    a, g = jnp.split(h @ w_in + b_in, 2, axis=-1)
    u = causal_dwconv(a * jax.nn.sigmoid(g), dw, dw_b)
    u32 = u.astype(jnp.float32)
    mu = jnp.mean(u32, axis=-1, keepdims=True)
    var = jnp.mean(jnp.square(u32 - mu), axis=-1, keepdims=True)
    u = ((u32 - mu) * lax.rsqrt(var + EPS) * ln_g.astype(jnp.float32) + ln_b.astype(jnp.float32)).astype(h.dtype)
    return jax.nn.silu(u) @ w_out + b_out


def multiscale_pool(h, w_grp, b_grp, scale):
    bsz, s, d = h.shape
    hg = h.reshape(bsz, s, N_POOL_GROUPS, POOL_GROUP_DIM).astype(jnp.float32)
    t = jnp.arange(s)
    outs = []
    for gi, w in enumerate(POOL_WINDOWS):
        xg = hg[:, :, gi]
        cs = jnp.cumsum(xg, axis=1)
        lag = jnp.pad(cs[:, :s - w], ((0, 0), (w, 0), (0, 0)))
        cnt = jnp.minimum(t + 1, w).astype(jnp.float32)[None, :, None]
        outs.append((cs - lag) / cnt - xg)
    p = jnp.stack(outs, axis=2).astype(h.dtype)
    y = jnp.einsum("bsgc,gcd->bsgd", p, w_grp) + b_grp
    return y.reshape(bsz, s, d) * scale


def fox_block_attention(q, k, v, c):
    bsz, nh, s, hd = q.shape
    nb = s // Q_BLOCK
    scale = 1.0 / math.sqrt(hd)
    qb = q.reshape(bsz, nh, nb, Q_BLOCK, hd).transpose(2, 0, 1, 3, 4)
    cb = c.reshape(bsz, nh, nb, Q_BLOCK).transpose(2, 0, 1, 3)
    pos = jnp.arange(s)
    qpos = pos.reshape(nb, Q_BLOCK)

    def one_block(args):
        qi, ci, pi = args
        logits = jnp.einsum("bhqd,bhkd->bhqk", qi, k).astype(jnp.float32) * scale
        logits = logits + ci[..., :, None] - c[:, :, None, :]
        mask = pi[:, None] >= pos[None, :]
        logits = jnp.where(mask[None, None], logits, -jnp.inf)
        probs = jax.nn.softmax(logits, axis=-1)
        return jnp.einsum("bhqk,bhkd->bhqd", probs.astype(v.dtype), v)

    out = lax.map(one_block, (qb, cb, qpos))
    return out.transpose(1, 2, 0, 3, 4).reshape(bsz, nh, s, hd)


def forgetting_attention(h, w_in, b_f, q_gain, k_gain, w_o):
    bsz, s, d = h.shape
    proj = h @ w_in
    q, k, v, fl = jnp.split(proj, [d, 2 * d, 3 * d], axis=-1)
    to_heads = lambda z: z.reshape(bsz, s, N_HEADS, HEAD_DIM).transpose(0, 2, 1, 3)
    q = rms_norm(to_heads(q), q_gain)
    k = rms_norm(to_heads(k), k_gain)
    v = to_heads(v)
    logf = jax.nn.log_sigmoid(fl.astype(jnp.float32) + b_f.astype(jnp.float32))
    c = jnp.cumsum(logf, axis=1).transpose(0, 2, 1)
    o = fox_block_attention(q, k, v, c)
    return o.transpose(0, 2, 1, 3).reshape(bsz, s, d) @ w_o


def conv_ffn(h, w_up, dw, dw_b, w_down):
    u = causal_dwconv(h @ w_up, dw, dw_b)
    val, gate = jnp.split(u, 2, axis=-1)
    return (jax.nn.silu(gate) * val) @ w_down


def setup_inputs(seed: int = 0) -> dict:
    key = jax.random.key(seed)
    ks = iter(jax.random.split(key, 32))
    nrm = lambda shape, s: jax.random.normal(next(ks), shape, jnp.float32) * s
    D, F = D_MODEL, D_FF
    return {
        "x": nrm((BATCH, SEQ, D), 1.0),
        "norm_mix": 1.0 + nrm((DEPTH, D), 0.02),
        "norm_ffn": 1.0 + nrm((DEPTH, D), 0.02),
        "conv_w_in": nrm((N_A, D, 2 * D), D ** -0.5),
        "conv_b_in": nrm((N_A, 2 * D), 0.02),
        "conv_dw": nrm((N_A, CONV_WIDTH, D), CONV_WIDTH ** -0.5),
        "conv_dw_b": nrm((N_A, D), 0.02),
        "conv_ln_g": 1.0 + nrm((N_A, D), 0.02),
        "conv_ln_b": nrm((N_A, D), 0.02),
        "conv_w_out": nrm((N_A, D, D), D ** -0.5),
        "conv_b_out": nrm((N_A, D), 0.02),
        "pool_w": nrm((N_B, N_POOL_GROUPS, POOL_GROUP_DIM, POOL_GROUP_DIM), POOL_GROUP_DIM ** -0.5),
        "pool_b": nrm((N_B, N_POOL_GROUPS, POOL_GROUP_DIM), 0.02),
        "pool_scale": 0.5 + nrm((N_B, D), 0.05),
        "fox_w_in": nrm((N_C, D, 3 * D + N_HEADS), D ** -0.5),
        "fox_b_f": 2.0 + nrm((N_C, N_HEADS), 0.5),
        "fox_q_gain": 1.0 + nrm((N_C, HEAD_DIM), 0.02),
        "fox_k_gain": 1.0 + nrm((N_C, HEAD_DIM), 0.02),
        "fox_w_o": nrm((N_C, D, D), D ** -0.5),
        "ffn_w_up": nrm((DEPTH, D, 2 * F), D ** -0.5),
        "ffn_dw": nrm((DEPTH, FFN_CONV_WIDTH, 2 * F), FFN_CONV_WIDTH ** -0.5),
        "ffn_dw_b": nrm((DEPTH, 2 * F), 0.02),
        "ffn_w_down": nrm((DEPTH, F, D), F ** -0.5),
    }


def reference(x, norm_mix, norm_ffn, conv_w_in, conv_b_in, conv_dw, conv_dw_b, conv_ln_g, conv_ln_b,
              conv_w_out, conv_b_out, pool_w, pool_b, pool_scale, fox_w_in, fox_b_f, fox_q_gain,
              fox_k_gain, fox_w_o, ffn_w_up, ffn_dw, ffn_dw_b, ffn_w_down):
    for i in range(DEPTH):
        j = i // N_MIXERS
        h = rms_norm(x, norm_mix[i])
        kind = i % N_MIXERS
        if kind == 0:
            y = conformer_conv(h, conv_w_in[j], conv_b_in[j], conv_dw[j], conv_dw_b[j],
                               conv_ln_g[j], conv_ln_b[j], conv_w_out[j], conv_b_out[j])
        elif kind == 1:
            y = multiscale_pool(h, pool_w[j], pool_b[j], pool_scale[j])
        else:
            y = forgetting_attention(h, fox_w_in[j], fox_b_f[j], fox_q_gain[j], fox_k_gain[j], fox_w_o[j])
        x = x + y
        x = x + conv_ffn(rms_norm(x, norm_ffn[i]), ffn_w_up[i], ffn_dw[i], ffn_dw_b[i], ffn_w_down[i])
    return x
```

```python
import os
import numpy as np
from contextlib import ExitStack
import concourse.bass as bass
import concourse.mybir as mybir
from concourse.bass_utils import run_bass_kernel_spmd

F32 = mybir.dt.float32
BF16 = mybir.dt.bfloat16
ALU = mybir.AluOpType
AF = mybir.ActivationFunctionType

D = 1024
NCH = 8
DFF = 2816
NF = 22
T = 512
NB = 4
EPS = 1e-6
CONVW = 31
NH = 16
HD = 64


class Buf:
    __slots__ = ("name", "w", "rs")

    def __init__(self, name):
        self.name = name
        self.w = None
        self.rs = {}


class Op:
    __slots__ = ("eng", "fn", "deps", "dwaits", "long", "dma_sem", "need_inc", "val")

    def __init__(self, eng, fn, long):
        self.eng = eng
        self.fn = fn
        self.long = long
        self.deps = {}
        self.dwaits = {}
        self.dma_sem = None
        self.need_inc = False
        self.val = 0


class Prog:
    CENG = ("pe", "act", "dve", "pool")
    ENG = ("pe", "act", "dve", "pool", "sp")

    def __init__(self, nc, stack, n_dma_sems=64):
        self.nc = nc
        self.stack = stack
        self.ops = {e: [] for e in self.ENG}
        self.esem = {e: stack.enter_context(nc.semaphore("s_" + e)) for e in self.CENG}
        self.dsem = [stack.enter_context(nc.semaphore("d%d" % i)) for i in range(n_dma_sems)]
        self.dcnt = [0] * n_dma_sems
        self.dnext = 0
        self.last = {}
        self.n_ops = 0
        self.uid = 0

    def sbuf(self, st, name, shape, dtype):
        self.uid += 1
        return st.enter_context(self.nc.sbuf_tensor("%s_%d" % (name, self.uid), list(shape), dtype))

    def psum(self, st, name, shape, dtype):
        self.uid += 1
        return st.enter_context(self.nc.psum_tensor("%s_%d" % (name, self.uid), list(shape), dtype))

    def new_sem(self):
        i = self.dnext
        self.dnext = (self.dnext + 1) % len(self.dsem)
        return i

    def _record(self, o, reads, writes):
        deps = o.deps
        dw = o.dwaits
        for b in reads:
            w = b.w
            if w is not None:
                if w.dma_sem is not None:
                    dw[w.dma_sem] = self.dcnt[w.dma_sem]
                else:
                    deps[w] = True
        for b in writes:
            w = b.w
            if w is not None:
                if w.dma_sem is not None:
                    dw[w.dma_sem] = self.dcnt[w.dma_sem]
                elif w not in deps:
                    deps[w] = False
            for r in b.rs.values():
                if r.dma_sem is not None:
                    dw[r.dma_sem] = self.dcnt[r.dma_sem]
                elif r not in deps:
                    deps[r] = False
        deps.pop(o, None)
        for b in writes:
            b.w = o
            b.rs = {}
        key = o.eng if o.dma_sem is None else ("d", o.dma_sem)
        for b in reads:
            if b.w is not o:
                b.rs[key] = o
        self.ops[o.eng].append(o)
        self.last[key] = o
        self.n_ops += 1
        return o

    def op(self, eng, fn, reads=(), writes=(), n=512):
        return self._record(Op(eng, fn, n >= 256), reads, writes)

    def dma(self, queue, out, in_, reads, writes, sem, **kw):
        o = Op(queue, lambda e: e.dma_start(out=out, in_=in_, **kw), True)
        o.dma_sem = sem
        self._record(o, reads, writes)
        self.dcnt[sem] += 16
        return o

    def barrier(self):
        lasts = list(self.last.values())
        for e in self.ENG:
            o = Op(e, None, True)
            for l in lasts:
                if l.dma_sem is not None:
                    o.dwaits[l.dma_sem] = self.dcnt[l.dma_sem]
                elif l.eng != e:
                    o.deps[l] = False
            self.ops[e].append(o)

    def emit(self):
        nc = self.nc
        for e in self.ENG:
            for o in self.ops[e]:
                for d, raw in o.deps.items():
                    if d.eng != o.eng:
                        d.need_inc = True
                    elif o.eng != "pe":
                        d.need_inc = True
        for e in self.CENG:
            c = 0
            for o in self.ops[e]:
                if o.need_inc:
                    c += 1
                    o.val = c
        esem, dsem = self.esem, self.dsem

        def run(e, eng):
            waited_e = {x: 0 for x in self.CENG}
            waited_d = {}
            for o in self.ops[e]:
                for d, raw in o.deps.items():
                    if d.eng == e and e == "pe":
                        continue
                    if waited_e[d.eng] < d.val:
                        eng.wait_ge(esem[d.eng], d.val)
                        waited_e[d.eng] = d.val
                for s, v in o.dwaits.items():
                    if waited_d.get(s, 0) < v:
                        eng.wait_ge(dsem[s], v)
                        waited_d[s] = v
                if o.fn is None:
                    continue
                ins = o.fn(eng)
                if o.dma_sem is not None:
                    ins.then_inc(dsem[o.dma_sem], 16)
                elif o.need_inc:
                    ins.then_inc(esem[e], 1)

        with nc.Block() as block:
            @block.tensor
            def _(eng):
                run("pe", eng)

            @block.scalar
            def _(eng):
                run("act", eng)

            @block.vector
            def _(eng):
                run("dve", eng)

            @block.gpsimd
            def _(eng):
                run("pool", eng)

            @block.sync
            def _(eng):
                run("sp", eng)


class Ctx:
    pass


def load_cols(P, st, name, rows_aps, pe_bank, pe_bank_buf, c):
    ident_f = c.idf
    nc = P.nc
    R = sum(a.shape[0] for a in rows_aps)
    cols = P.sbuf(st, name, [128, R], F32)
    cb = Buf(name)
    done = 0
    grp = []
    stage_i = 0
    pending = []
    cur = 0
    for a in rows_aps:
        r0 = 0
        while r0 < a.shape[0]:
            take = min(a.shape[0] - r0, 128 - cur)
            pending.append((cur, a[r0:r0 + take, :], take))
            cur += take
            r0 += take
            if cur == 128:
                grp.append(pending)
                pending = []
                cur = 0
    if pending:
        grp.append(pending)
    for gi, pend in enumerate(grp):
        nrows = sum(t for _, _, t in pend)
        stg = P.sbuf(st, "%s_stg%d" % (name, gi), [128, 128], F32)
        sb = Buf("stg")
        sem = P.new_sem()
        for (p0, ap, take) in pend:
            P.dma("sp", stg[p0:p0 + take, :], ap, [], [sb], sem)
        P.op("pe", lambda e, stg=stg, nrows=nrows: e.transpose(out=pe_bank[:, 0:nrows], in_=stg[0:nrows, :], identity=ident_f[0:nrows, 0:nrows]),
             [sb, c.const_b], [pe_bank_buf], n=128)
        P.op("dve", lambda e, done=done, nrows=nrows: e.tensor_copy(out=cols[:, done:done + nrows], in_=pe_bank[:, 0:nrows]),
             [pe_bank_buf], [cb], n=nrows)
        done += nrows
    return cols, cb


def make_identity(P, st):
    nc = P.nc
    idf = P.sbuf(st, "ident_f", [128, 128], F32)
    idb = P.sbuf(st, "ident_b", [128, 128], BF16)
    b = Buf("ident")
    P.op("pool", lambda e: e.memset(idf[:], 1.0), [], [b], n=128)
    P.op("pool", lambda e: e.affine_select(out=idf[:], in_=idf[:], pattern=[[-1, 128]], compare_op=ALU.is_equal,
                                           fill=0.0, base=0, channel_multiplier=1), [b], [b], n=128)
    P.op("pool", lambda e: e.tensor_copy(out=idb[:], in_=idf[:]), [b], [b], n=128)
    return idf, idb, b


def alloc_norm(P, st, c, g_dram_row, want_hT=True, n_xnb=2):
    N = Ctx()
    N.xin = [P.sbuf(st, "xin%d" % i, [128, D], F32) for i in range(2)]
    N.xin_b = [Buf("xin%d" % i) for i in range(2)]
    N.xin_sem = [P.new_sem() for _ in range(2)]
    N.xnb = [P.sbuf(st, "xnb%d" % i, [128, D], BF16) for i in range(n_xnb)]
    N.xnb_b = [Buf("xnb%d" % i) for i in range(n_xnb)]
    N.n_xnb = n_xnb
    N.cnt_a = 0
    N.ss = [P.sbuf(st, "ss%d" % i, [128, 4], F32) for i in range(2)]
    N.ss_b = [Buf("ss%d" % i) for i in range(2)]
    N.gbc = P.sbuf(st, "gbc", [128, D], F32)
    N.gbc_b = Buf("gbc")
    P.dma("sp", N.gbc[:], g_dram_row.partition_broadcast(128), [], [N.gbc_b], P.new_sem())
    if want_hT:
        N.hT = P.sbuf(st, "hT", [128, NCH, T], BF16)
        N.hT_b = [Buf("hT%d" % i) for i in range(NB)]
    if want_hT:
        N.tp = P.psum(st, "tp", [128, D], BF16)
        N.tp_b = Buf("tp")
    N.cnt = 0
    return N


def norm_block(P, c, N, src_ap, b, out_ap=None, out_bufs=None):
    sx = norm_part_a(P, c, N, src_ap)
    norm_part_b(P, c, N, sx, b, out_ap, out_bufs)


def tile_list(c):
    return [(s, i) for s in range(c.n_seq) for i in range(c.S // T)]


def norm_part_a(P, c, N, src_ap):
    s = N.cnt % 2
    N.cnt += 1
    sx = N.cnt_a % N.n_xnb
    N.cnt_a += 1
    xin, xnb, ss = N.xin[s], N.xnb[sx], N.ss[s]
    xnb_b = N.xnb_b[sx]
    P.dma("sp", xin[:], src_ap, [], [N.xin_b[s]], N.xin_sem[s])
    P.op("act", lambda e: e.activation(out=xnb[:], in_=xin[:], func=AF.Square, accum_out=ss[:, 0:1]),
         [N.xin_b[s]], [xnb_b, N.ss_b[s]], n=D)
    P.op("pool", lambda e: e.tensor_scalar(out=ss[:, 1:2], in0=ss[:, 0:1], scalar1=1.0 / D, scalar2=EPS,
                                           op0=ALU.mult, op1=ALU.add), [N.ss_b[s]], [N.ss_b[s]], n=1)
    P.op("pool", lambda e: e.tensor_tensor(out=ss[:, 2:3], in0=ss[:, 1:2], in1=c.nhalf[:, 0:1], op=ALU.pow),
         [N.ss_b[s], c.const_b], [N.ss_b[s]], n=1)
    P.op("dve", lambda e: e.scalar_tensor_tensor(out=xnb[:], in0=xin[:], scalar=ss[:, 2:3], in1=N.gbc[:],
                                                 op0=ALU.mult, op1=ALU.mult),
         [N.xin_b[s], N.ss_b[s], N.gbc_b], [xnb_b], n=D)
    return sx


def norm_part_b(P, c, N, s, b, out_ap=None, out_bufs=None):
    xnb = N.xnb[s]
    for k in range(NCH):
        P.op("pe", lambda e, k=k: e.transpose(out=N.tp[:, k * 128:(k + 1) * 128], in_=xnb[:, k * 128:(k + 1) * 128],
                                              identity=c.idb[:]),
             [N.xnb_b[s], c.const_b], [N.tp_b], n=128)
    if out_ap is None:
        out_ap = N.hT[:, :, b * 128:(b + 1) * 128]
        out_bufs = [N.hT_b[b]]
    P.op("act", lambda e: e.copy(out=out_ap, in_=N.tp[:].rearrange("p (k t) -> p k t", k=NCH)),
         [N.tp_b], out_bufs, n=D)


def ffn_phase(P, c, li, src, dst):
    nc = P.nc
    with ExitStack() as st:
        N = alloc_norm(P, st, c, c.w["norm_ffn"][li:li + 1, :])
        Wup = [P.sbuf(st, "wup%d" % k, [128, 2 * DFF], BF16) for k in range(NCH)]
        Wup_b = [Buf("wup%d" % k) for k in range(NCH)]
        for k in range(NCH):
            P.dma("pool", Wup[k][:], c.w["ffn_w_up"][li, k * 128:(k + 1) * 128, :], [], [Wup_b[k]], P.new_sem(),
                  max_dma_last_dim=4096)
        Wdn = P.sbuf(st, "wdn", [128, NF, D], BF16)
        Wdn_b = [Buf("wdn0"), Buf("wdn1")]
        wd_view = c.w["ffn_w_down"][li].rearrange("(j p) d -> p j d", p=128)
        P.dma("pool", Wdn[:, 0:11, :], wd_view[:, 0:11, :], [], [Wdn_b[0]], P.new_sem(), max_dma_last_dim=4096)
        P.dma("pool", Wdn[:, 11:22, :], wd_view[:, 11:22, :], [], [Wdn_b[1]], P.new_sem(), max_dma_last_dim=4096)
        pu = [[P.psum(st, "pu%d%d" % (s, v), [128, T], F32) for v in range(2)] for s in range(2)]
        pu_b = [[Buf("pu") for v in range(2)] for s in range(2)]
        pd = [P.psum(st, "pd%d" % i, [128, T], F32) for i in range(3)]
        pd_b = [Buf("pd") for i in range(3)]
        cols, cols_b = load_cols(P, st, "fcols",
                                 [c.w["ffn_dw"][li].rearrange("k (c p) -> (k c) p", p=128),
                                  c.w["ffn_dw_b"][li].rearrange("(c p) -> c p", p=128)],
                                 pd[0], pd_b[0], c)
        NC2 = 2 * NF

        def wcol(k, ch):
            return cols[:, k * NC2 + ch:k * NC2 + ch + 1]

        def bcol(ch):
            return cols[:, 3 * NC2 + ch:3 * NC2 + ch + 1]

        cv = [P.sbuf(st, "cv%d" % s, [128, 2, T], F32) for s in range(2)]
        cv_b = [[Buf("cv"), Buf("cg")] for s in range(2)]
        hs = [P.sbuf(st, "hs%d" % i, [128, NC2, 2], F32) for i in range(2)]
        hs_b = [Buf("hs%d" % i) for i in range(2)]
        fix = [P.sbuf(st, "fix%d" % i, [128, NC2, 2], F32) for i in range(2)]
        fix_b = [Buf("fix%d" % i) for i in range(2)]
        ftmp = P.sbuf(st, "ftmp", [128, NC2], F32)
        ftmp_b = Buf("ftmp")
        gT = P.sbuf(st, "gT", [128, NF, T], BF16)
        gT_b = [Buf("gT%d" % j) for j in range(NF)]
        NXR = 3
        xres = [P.sbuf(st, "xres%d" % i, [128, D], F32) for i in range(NXR)]
        xres_b = [Buf("xres%d" % i) for i in range(NXR)]
        xres_sem = [P.new_sem() for _ in range(NXR)]
        tiles = tile_list(c)

        def blk_src(ti, b):
            s, i = tiles[ti]
            t0 = i * T + b * 128
            return src[s, t0:t0 + 128, :]

        def stageB(j):
            sl = j % 2
            P.op("act", lambda e: e.activation(out=cv[sl][:, 1, :], in_=cv[sl][:, 1, :], func=AF.Silu),
                 [cv_b[sl][1]], [cv_b[sl][1]])
            P.op("pool", lambda e: e.tensor_tensor(out=gT[:, j, :], in0=cv[sl][:, 1, :], in1=cv[sl][:, 0, :], op=ALU.mult),
                 [cv_b[sl][0], cv_b[sl][1]], [gT_b[j]])

        for b in range(NB):
            sa = norm_part_a(P, c, N, blk_src(0, b))
            norm_part_b(P, c, N, sa, b)
        rcnt = 0
        for ti, (s, i) in enumerate(tiles):
            par = ti % 2
            nxt = ti + 1 < len(tiles)
            pend = []
            for j in range(NF):
                sl = j % 2
                for v in range(2):
                    ch = v * NF + j
                    for k in range(NCH):
                        P.op("pe", lambda e, v=v, k=k, ch=ch, sl=sl: e.matmul(
                            pu[sl][v][:], lhsT=Wup[k][:, ch * 128:(ch + 1) * 128], rhs=N.hT[:, k, :],
                            start=(k == 0), stop=(k == NCH - 1)),
                            [Wup_b[k]] + N.hT_b, [pu_b[sl][v]])
                for v in range(2):
                    ch = v * NF + j
                    P.op("act", lambda e, v=v, sl=sl, ch=ch: e.activation(
                        out=cv[sl][:, v, :], in_=pu[sl][v][:], func=AF.Identity, scale=wcol(2, ch), bias=bcol(ch)),
                        [pu_b[sl][v], cols_b], [cv_b[sl][v]])
                if i != 0:
                    for v in range(2):
                        ch = v * NF + j
                        P.op("pool", lambda e, v=v, sl=sl, ch=ch, par=par: e.tensor_tensor(
                            out=cv[sl][:, v, 0:2], in0=cv[sl][:, v, 0:2], in1=fix[1 - par][:, ch, :], op=ALU.add),
                            [cv_b[sl][v], fix_b[1 - par]], [cv_b[sl][v]], n=2)
                for k in (1, 0):
                    for v in range(2):
                        ch = v * NF + j
                        sh = 2 - k
                        P.op("dve", lambda e, v=v, sl=sl, ch=ch, k=k, sh=sh: e.scalar_tensor_tensor(
                            out=cv[sl][:, v, sh:T], in0=pu[sl][v][:, 0:T - sh], scalar=wcol(k, ch), in1=cv[sl][:, v, sh:T],
                            op0=ALU.mult, op1=ALU.add),
                            [pu_b[sl][v], cv_b[sl][v], cols_b], [cv_b[sl][v]])
                for v in range(2):
                    ch = v * NF + j
                    P.op("dve", lambda e, v=v, sl=sl, ch=ch, par=par: e.tensor_copy(out=hs[par][:, ch, :],
                                                                                   in_=pu[sl][v][:, T - 2:T]),
                         [pu_b[sl][v], cv_b[sl][v]], [hs_b[par]], n=2)
                if j >= 1:
                    stageB(j - 1)
                if nxt and j in (12, 17):
                    pend.append(norm_part_a(P, c, N, blk_src(ti + 1, len(pend))))
            stageB(NF - 1)
            if nxt and tiles[ti + 1][1] != 0:
                P.op("dve", lambda e, par=par: e.tensor_tensor(out=fix[par][:, :, 0], in0=hs[par][:, :, 1],
                                                                in1=cols[:, NC2:2 * NC2], op=ALU.mult),
                     [hs_b[par], cols_b], [fix_b[par]], n=NC2)
                P.op("dve", lambda e, par=par: e.tensor_tensor(out=ftmp[:], in0=hs[par][:, :, 0], in1=cols[:, 0:NC2],
                                                                op=ALU.mult), [hs_b[par], cols_b], [ftmp_b], n=NC2)
                P.op("dve", lambda e, par=par: e.tensor_tensor(out=fix[par][:, :, 0], in0=fix[par][:, :, 0], in1=ftmp[:],
                                                                op=ALU.add), [fix_b[par], ftmp_b], [fix_b[par]], n=NC2)
                P.op("dve", lambda e, par=par: e.tensor_tensor(out=fix[par][:, :, 1], in0=hs[par][:, :, 1],
                                                                in1=cols[:, 0:NC2], op=ALU.mult),
                     [hs_b[par], cols_b], [fix_b[par]], n=NC2)
            rsl = []
            for b in range(2):
                rs = rcnt % NXR
                rcnt += 1
                rsl.append(rs)
                P.dma("sp", xres[rs][:], blk_src(ti, b), [], [xres_b[rs]], xres_sem[rs])
            for b in range(NB):
                rs = rsl[b]
                for o in range(2):
                    q = (2 * b + o) % 3
                    for j in range(NF):
                        P.op("pe", lambda e, j=j, b=b, o=o, q=q: e.matmul(
                            pd[q][:], lhsT=gT[:, j, b * 128:(b + 1) * 128], rhs=Wdn[:, j, o * 512:(o + 1) * 512],
                            start=(j == 0), stop=(j == NF - 1)),
                            [gT_b[j], Wdn_b[0 if j < 11 else 1]], [pd_b[q]])
                    P.op("dve", lambda e, rs=rs, o=o, q=q: e.tensor_tensor(
                        out=xres[rs][:, o * 512:(o + 1) * 512], in0=pd[q][:], in1=xres[rs][:, o * 512:(o + 1) * 512],
                        op=ALU.add), [pd_b[q], xres_b[rs]], [xres_b[rs]])
                    if nxt and o == 0:
                        norm_part_b(P, c, N, pend[b], b)
                        if b + 2 < NB:
                            pend.append(norm_part_a(P, c, N, blk_src(ti + 1, b + 2)))
                t0 = i * T + b * 128
                P.dma("sp", dst[s, t0:t0 + 128, :], xres[rs][:], [xres_b[rs]], [], xres_sem[rs])
                if b + 2 < NB:
                    rs2 = rcnt % NXR
                    rcnt += 1
                    rsl.append(rs2)
                    P.dma("sp", xres[rs2][:], blk_src(ti, b + 2), [], [xres_b[rs2]], xres_sem[rs2])
        P.barrier()


def pool_phase(P, c, li, src, dst):
    j = li // 3
    with ExitStack() as st:
        N = alloc_norm(P, st, c, c.w["norm_mix"][li:li + 1, :], want_hT=False, n_xnb=6)
        Wp = P.sbuf(st, "wp", [128, 4, 2, 256], BF16)
        Wp_b = Buf("wp")
        P.dma("pool", Wp[:], c.w["pool_w"][j].rearrange("g (kk p) d -> p g kk d", p=128), [], [Wp_b], P.new_sem())
        sc = P.sbuf(st, "sc_bc", [128, D], F32)
        bs = P.sbuf(st, "bs_bc", [128, D], F32)
        sc_b, bs_b = Buf("sc"), Buf("bs")
        P.dma("sp", sc[:], c.w["pool_scale"][j:j + 1, :].partition_broadcast(128), [], [sc_b], P.new_sem())
        P.dma("sp", bs[:], c.w["pool_b"][j:j + 1].rearrange("o g c -> o (g c)").partition_broadcast(128), [], [bs_b],
              P.new_sem())
        P.op("pool", lambda e: e.tensor_tensor(out=bs[:], in0=bs[:], in1=sc[:], op=ALU.mult), [sc_b, bs_b], [bs_b], n=D)
        Bcur = P.sbuf(st, "Bcur", [128, 4, 128], BF16)
        Bprev = P.sbuf(st, "Bprev", [128, 4, 128], BF16)
        Bfirst = P.sbuf(st, "Bfirst", [128, 4, 128], BF16)
        band_b = Buf("band")
        tmpf = P.sbuf(st, "band_tmp", [128, 128], F32)
        invr = P.sbuf(st, "band_inv", [128, 128], F32)
        tmp_b2 = Buf("band_tmp")
        for g in range(4):
            w = 2 << g
            P.op("pool", lambda e, w=w: e.memset(tmpf[:], 1.0 / w), [], [tmp_b2], n=128)
            P.op("pool", lambda e: e.affine_select(out=tmpf[:], in_=tmpf[:], pattern=[[1, 128]], compare_op=ALU.is_ge,
                                                   fill=0.0, base=0, channel_multiplier=-1), [tmp_b2], [tmp_b2], n=128)
            P.op("pool", lambda e, w=w: e.affine_select(out=tmpf[:], in_=tmpf[:], pattern=[[-1, 128]], compare_op=ALU.is_ge,
                                                        fill=0.0, base=w - 1, channel_multiplier=1), [tmp_b2], [tmp_b2], n=128)
            P.op("pool", lambda e, g=g: e.tensor_tensor(out=Bcur[:, g, :], in0=tmpf[:], in1=c.idf[:], op=ALU.subtract),
                 [tmp_b2, c.const_b], [band_b], n=128)
            P.op("pool", lambda e, w=w: e.memset(tmpf[:], 1.0 / w), [band_b], [tmp_b2], n=128)
            P.op("pool", lambda e, w=w: e.affine_select(out=tmpf[:], in_=tmpf[:], pattern=[[-1, 128]], compare_op=ALU.is_ge,
                                                        fill=0.0, base=-(129 - w), channel_multiplier=1),
                 [tmp_b2], [tmp_b2], n=128)
            P.op("pool", lambda e, g=g: e.tensor_copy(out=Bprev[:, g, :], in_=tmpf[:]), [tmp_b2], [band_b], n=128)
            P.op("pool", lambda e: e.memset(tmpf[:], 1.0), [band_b], [tmp_b2], n=128)
            P.op("pool", lambda e: e.affine_select(out=tmpf[:], in_=tmpf[:], pattern=[[1, 128]], compare_op=ALU.is_ge,
                                                   fill=0.0, base=0, channel_multiplier=-1), [tmp_b2], [tmp_b2], n=128)
            P.op("pool", lambda e, w=w: e.affine_select(out=tmpf[:], in_=tmpf[:], pattern=[[-1, 128]], compare_op=ALU.is_ge,
                                                        fill=0.0, base=w - 1, channel_multiplier=1), [tmp_b2], [tmp_b2], n=128)
            P.op("pool", lambda e, w=w: e.memset(invr[:], 1.0 / w), [tmp_b2], [tmp_b2], n=128)
            for t in range(w - 1):
                P.op("pool", lambda e, t=t: e.memset(invr[:, t:t + 1], 1.0 / (t + 1)), [tmp_b2], [tmp_b2], n=1)
            P.op("pool", lambda e: e.tensor_tensor(out=tmpf[:], in0=tmpf[:], in1=invr[:], op=ALU.mult), [tmp_b2], [tmp_b2],
                 n=128)
            P.op("pool", lambda e, g=g: e.tensor_tensor(out=Bfirst[:, g, :], in0=tmpf[:], in1=c.idf[:], op=ALU.subtract),
                 [tmp_b2, c.const_b], [band_b], n=128)
        pp = [P.psum(st, "pp%d" % i, [128, D], F32) for i in range(2)]
        pp_b = [Buf("pp") for i in range(2)]
        po = [P.psum(st, "po%d" % i, [128, D], F32) for i in range(2)]
        po_b = [Buf("po") for i in range(2)]
        pT = [P.sbuf(st, "pT%d" % i, [128, NCH, 128], BF16) for i in range(2)]
        pT_b = [Buf("pT%d" % i) for i in range(2)]
        NXR = 4
        xres = [P.sbuf(st, "xres%d" % i, [128, D], F32) for i in range(NXR)]
        xres_b = [Buf("xres%d" % i) for i in range(NXR)]
        xres_sem = [P.new_sem() for _ in range(NXR)]
        tmp = [P.sbuf(st, "ptmp%d" % i, [128, D], F32) for i in range(2)]
        tmp_b = [Buf("ptmp") for i in range(2)]
        blocks = [(s, ib) for s in range(c.n_seq) for ib in range(c.S // 128)]

        def part_a(n):
            s, ib = blocks[n]
            return norm_part_a(P, c, N, src[s, ib * 128:(ib + 1) * 128, :])

        LA = 4
        pend_store = []

        def flush_store():
            s_, ib_, rs_ = pend_store.pop(0)
            P.dma("act", dst[s_, ib_ * 128:(ib_ + 1) * 128, :], xres[rs_][:], [xres_b[rs_]], [], xres_sem[rs_])

        slots = {}
        for n in range(min(LA, len(blocks))):
            slots[n] = part_a(n)
        for n, (s, ib) in enumerate(blocks):
            if n + LA < len(blocks):
                slots[n + LA] = part_a(n + LA)
            sl = n % 2
            rs = n % NXR
            xc = N.xnb[slots[n]]
            xc_b = N.xnb_b[slots[n]]
            P.dma("sp", xres[rs][:], src[s, ib * 128:(ib + 1) * 128, :], [], [xres_b[rs]], xres_sem[rs])
            P.op("pool", lambda e, rs=rs: e.tensor_tensor(out=xres[rs][:], in0=xres[rs][:], in1=bs[:], op=ALU.add),
                 [xres_b[rs], bs_b], [xres_b[rs]], n=D)
            for k in range(NCH):
                g = k // 2
                if ib == 0:
                    P.op("pe", lambda e, k=k, g=g, sl=sl, xc=xc: e.matmul(
                        pp[sl][:, k * 128:(k + 1) * 128], lhsT=xc[:, k * 128:(k + 1) * 128], rhs=Bfirst[:, g, :],
                        start=True, stop=True), [xc_b, band_b], [pp_b[sl]], n=128)
                else:
                    xp = N.xnb[slots[n - 1]]
                    xp_b = N.xnb_b[slots[n - 1]]
                    P.op("pe", lambda e, k=k, g=g, sl=sl, xc=xc: e.matmul(
                        pp[sl][:, k * 128:(k + 1) * 128], lhsT=xc[:, k * 128:(k + 1) * 128], rhs=Bcur[:, g, :],
                        start=True, stop=False), [xc_b, band_b], [pp_b[sl]], n=128)
                    P.op("pe", lambda e, k=k, g=g, sl=sl, xp=xp: e.matmul(
                        pp[sl][:, k * 128:(k + 1) * 128], lhsT=xp[:, k * 128:(k + 1) * 128], rhs=Bprev[:, g, :],
                        start=False, stop=True), [xp_b, band_b], [pp_b[sl]], n=128)
            P.op("act", lambda e, sl=sl: e.copy(out=pT[sl][:], in_=pp[sl][:].rearrange("p (k t) -> p k t", k=NCH)),
                 [pp_b[sl]], [pT_b[sl]], n=D)
            for g in range(4):
                for kk in range(2):
                    P.op("pe", lambda e, g=g, kk=kk, sl=sl: e.matmul(
                        po[sl][:, g * 256:(g + 1) * 256], lhsT=pT[sl][:, 2 * g + kk, :], rhs=Wp[:, g, kk, :],
                        start=(kk == 0), stop=(kk == 1)), [pT_b[sl], Wp_b], [po_b[sl]])
            P.op("dve", lambda e, sl=sl: e.tensor_tensor(out=tmp[sl][:], in0=po[sl][:], in1=sc[:], op=ALU.mult),
                 [po_b[sl], sc_b], [tmp_b[sl]], n=D)
            P.op("pool", lambda e, rs=rs, sl=sl: e.tensor_tensor(out=xres[rs][:], in0=xres[rs][:], in1=tmp[sl][:], op=ALU.add),
                 [xres_b[rs], tmp_b[sl]], [xres_b[rs]], n=D)
            pend_store.append((s, ib, rs))
            if len(pend_store) > 1:
                flush_store()
        while pend_store:
            flush_store()
        P.barrier()


def conv_phase(P, c, li, src, dst):
    j = li // 3
    HS = CONVW - 1
    with ExitStack() as st:
        N = alloc_norm(P, st, c, c.w["norm_mix"][li:li + 1, :], n_xnb=4)
        Win = [P.sbuf(st, "win%d" % k, [128, 2 * D], BF16) for k in range(NCH)]
        Win_b = [Buf("win%d" % k) for k in range(NCH)]
        for k in range(NCH):
            P.dma("pool", Win[k][:], c.w["conv_w_in"][j, k * 128:(k + 1) * 128, :], [], [Win_b[k]], P.new_sem(),
                  max_dma_last_dim=4096)
        Wout = P.sbuf(st, "wout", [128, NCH, D], BF16)
        Wout_b = Buf("wout")
        P.dma("pool", Wout[:], c.w["conv_w_out"][j].rearrange("(k p) d -> p k d", p=128), [], [Wout_b], P.new_sem(),
              max_dma_last_dim=4096)
        bo = P.sbuf(st, "bo_bc", [128, D], F32)
        bo_b = Buf("bo")
        P.dma("sp", bo[:], c.w["conv_b_out"][j:j + 1, :].partition_broadcast(128), [], [bo_b], P.new_sem())
        B = [P.psum(st, "B%d" % i, [128, T], F32) for i in range(7)]
        B_b = [Buf("B%d" % i) for i in range(7)]
        cols, cols_b = load_cols(P, st, "ccols",
                                 [c.w["conv_b_in"][j].rearrange("(c p) -> c p", p=128),
                                  c.w["conv_dw"][j].rearrange("k (c p) -> (k c) p", p=128),
                                  c.w["conv_dw_b"][j].rearrange("(c p) -> c p", p=128),
                                  c.w["conv_ln_g"][j].rearrange("(c p) -> c p", p=128),
                                  c.w["conv_ln_b"][j].rearrange("(c p) -> c p", p=128)],
                                 B[0], B_b[0], c)
        col = lambda i: cols[:, i:i + 1]
        bin_col = lambda ch: col(ch)
        dw_col = lambda k, ch: col(16 + k * NCH + ch)
        dwb_col = lambda ch: col(16 + CONVW * NCH + ch)
        lng_col = lambda ch: col(16 + CONVW * NCH + 8 + ch)
        lnb_col = lambda ch: col(16 + CONVW * NCH + 16 + ch)
        Dg = P.sbuf(st, "Dg", [128, NCH, CONVW, 128], BF16)
        Dg_b = [[Buf("Dg") for k in range(CONVW)] for ch in range(NCH)]

        def build_dg():
            n = 0
            for ch in range(NCH):
                for k in range(CONVW):
                    eng = "dve" if n % 2 == 0 else "pool"
                    n += 1
                    if eng == "dve":
                        P.op(eng, lambda e, ch=ch, k=k: e.tensor_scalar(out=Dg[:, ch, k, :], in0=c.idb[:], scalar1=dw_col(k, ch),
                                                                        scalar2=None, op0=ALU.mult),
                             [c.const_b, cols_b], [Dg_b[ch][k]], n=128)
                    else:
                        P.op(eng, lambda e, ch=ch, k=k: e.tensor_scalar(out=Dg[:, ch, k, :], in0=c.idb[:], scalar1=dw_col(k, ch),
                                                                        scalar2=1.0, op0=ALU.mult, op1=ALU.mult),
                             [c.const_b, cols_b], [Dg_b[ch][k]], n=128)
        onesM = P.sbuf(st, "onesM", [128, 128], BF16)
        onesM_b = Buf("onesM")
        P.op("pool", lambda e: e.memset(onesM[:], 1.0 / D), [], [onesM_b], n=128)
        Vb = P.sbuf(st, "Vb", [128, NCH, HS + T], BF16)
        Vh_b = Buf("Vh")
        Vm_b = [Buf("Vm%d" % k) for k in range(NCH)]
        cub = P.sbuf(st, "cub", [128, NCH, T], BF16)
        cub_b = [Buf("cub%d" % k) for k in range(NCH)]
        sq = P.sbuf(st, "sq", [128, NCH, T], BF16)
        sq_b = [Buf("sq%d" % k) for k in range(NCH)]
        sT = P.sbuf(st, "sT", [128, NCH, T], BF16)
        sT_b = [Buf("sT%d" % k) for k in range(NCH)]
        sgt = [P.sbuf(st, "sgt%d" % i, [128, T], F32) for i in range(2)]
        sgt_b = [Buf("sgt") for i in range(2)]
        mean_sb = P.sbuf(st, "mean_sb", [128, T], F32)
        var = P.sbuf(st, "var", [128, T], F32)
        rstd = P.sbuf(st, "rstd", [128, T], F32)
        mean_b, var_b, rstd_b = Buf("mean"), Buf("var"), Buf("rstd")
        t1 = [P.sbuf(st, "t1%d" % i, [128, T], F32) for i in range(2)]
        t1_b = [Buf("t1") for i in range(2)]
        sg2 = [P.sbuf(st, "sg2%d" % i, [128, T], F32) for i in range(2)]
        sg2_b = [Buf("sg2") for i in range(2)]
        NXR = 4
        xres = [P.sbuf(st, "xres%d" % i, [128, 512], F32) for i in range(NXR)]
        xres_b = [Buf("xres%d" % i) for i in range(NXR)]
        xres_sem = [P.new_sem() for _ in range(NXR)]
        tiles = tile_list(c)
        rc = [0]

        def norm_a(ti):
            s, i = tiles[ti]
            return [norm_part_a(P, c, N, src[s, i * T + b * 128:i * T + (b + 1) * 128, :]) for b in range(NB)]

        def norm_b(slots):
            for b in range(NB):
                norm_part_b(P, c, N, slots[b], b)

        def in_head(ti):
            s, i = tiles[ti]
            if i == 0:
                P.op("pool", lambda e: e.memset(Vb[:, :, 0:HS], 0.0), [], [Vh_b], n=HS * NCH)

        def in_pair(ti, ch):
            if True:
                sl = ch % 2
                pa, pg = B[2 * sl], B[2 * sl + 1]
                for v, pp in ((0, pa), (1, pg)):
                    cc = v * NCH + ch
                    for k in range(NCH):
                        P.op("pe", lambda e, pp=pp, k=k, cc=cc: e.matmul(
                            pp[:], lhsT=Win[k][:, cc * 128:(cc + 1) * 128], rhs=N.hT[:, k, :],
                            start=(k == 0), stop=(k == NCH - 1)), [Win_b[k]] + N.hT_b, [B_b[2 * sl + v]])
                P.op("act", lambda e, sl=sl, pg=pg, ch=ch: e.activation(out=sgt[sl][:], in_=pg[:], func=AF.Sigmoid,
                                                                        bias=bin_col(NCH + ch), scale=1.0),
                     [B_b[2 * sl + 1], cols_b], [sgt_b[sl]])
                P.op("dve", lambda e, sl=sl, pa=pa, ch=ch: e.scalar_tensor_tensor(
                    out=Vb[:, ch, HS:HS + T], in0=pa[:], scalar=bin_col(ch), in1=sgt[sl][:], op0=ALU.add, op1=ALU.mult),
                    [B_b[2 * sl], sgt_b[sl], cols_b], [Vm_b[ch]])

        def stage_conv(ti):
            s, i = tiles[ti]
            for ch in range(NCH):
                pc = B[4 + ch % 2]
                pcb = B_b[4 + ch % 2]
                for k in range(CONVW):
                    P.op("pe", lambda e, pc=pc, ch=ch, k=k: e.matmul(
                        pc[:], lhsT=Dg[:, ch, k, :], rhs=Vb[:, ch, k:k + T], start=(k == 0), stop=(k == CONVW - 1)),
                        [Dg_b[ch][k], Vh_b, Vm_b[ch]], [pcb])
                P.op("act", lambda e, pc=pc, ch=ch: e.activation(out=cub[:, ch, :], in_=pc[:], func=AF.Identity,
                                                                 bias=dwb_col(ch), scale=1.0),
                     [pcb, cols_b], [cub_b[ch]])
                P.op("act", lambda e, pc=pc, ch=ch: e.activation(out=sq[:, ch, :], in_=pc[:], func=AF.Square,
                                                                 bias=dwb_col(ch), scale=1.0),
                     [pcb, cols_b], [sq_b[ch]])
            if ti + 1 < len(tiles) and tiles[ti + 1][1] != 0:
                P.op("pool", lambda e: e.tensor_copy(out=Vb[:, :, 0:HS], in_=Vb[:, :, T:T + HS]), Vm_b, [Vh_b], n=HS * NCH)

        def stage_stats(ti):
            pm, pq = B[6], B[0]
            for ch in range(NCH):
                P.op("pe", lambda e, ch=ch: e.matmul(pm[:], lhsT=onesM[:], rhs=cub[:, ch, :], start=(ch == 0),
                                                     stop=(ch == NCH - 1)), [onesM_b, cub_b[ch]], [B_b[6]])
            for ch in range(NCH):
                P.op("pe", lambda e, ch=ch: e.matmul(pq[:], lhsT=onesM[:], rhs=sq[:, ch, :], start=(ch == 0),
                                                     stop=(ch == NCH - 1)), [onesM_b, sq_b[ch]], [B_b[0]])
            P.op("act", lambda e: e.copy(out=mean_sb[:], in_=pm[:]), [B_b[6]], [mean_b])
            P.op("act", lambda e: e.activation(out=var[:], in_=pm[:], func=AF.Square), [B_b[6]], [var_b])
            P.op("dve", lambda e: e.tensor_tensor(out=var[:], in0=pq[:], in1=var[:], op=ALU.subtract), [B_b[0], var_b], [var_b])
            P.op("act", lambda e: e.activation(out=var[:], in_=var[:], func=AF.Sqrt, bias=c.eps_col[:, 0:1], scale=1.0),
                 [var_b, c.const_b], [var_b])
            P.op("dve", lambda e: e.reciprocal(out=rstd[:], in_=var[:]), [var_b], [rstd_b])

        def ln_chunk(ti, ch):
            if True:
                sl = ch % 2
                P.op("pool", lambda e, sl=sl, ch=ch: e.tensor_tensor(out=t1[sl][:], in0=cub[:, ch, :], in1=mean_sb[:],
                                                                     op=ALU.subtract), [cub_b[ch], mean_b], [t1_b[sl]])
                P.op("dve", lambda e, sl=sl: e.tensor_tensor(out=t1[sl][:], in0=t1[sl][:], in1=rstd[:], op=ALU.mult),
                     [t1_b[sl], rstd_b], [t1_b[sl]])
                P.op("act", lambda e, sl=sl, ch=ch: e.activation(out=t1[sl][:], in_=t1[sl][:], func=AF.Identity,
                                                                 scale=lng_col(ch), bias=lnb_col(ch)),
                     [t1_b[sl], cols_b], [t1_b[sl]])
                P.op("act", lambda e, sl=sl, ch=ch: e.activation(out=sg2[sl][:], in_=t1[sl][:], func=AF.Sigmoid),
                     [t1_b[sl]], [sg2_b[sl]])
                P.op("pool", lambda e, sl=sl, ch=ch: e.tensor_tensor(out=sT[:, ch, :], in0=t1[sl][:], in1=sg2[sl][:],
                                                                     op=ALU.mult), [t1_b[sl], sg2_b[sl]], [sT_b[ch]])

        def stage_out(ti):
            s, i = tiles[ti]
            grp = [(b, o) for b in range(NB) for o in range(2)]

            def ld(gi):
                b, o = grp[gi]
                t0 = i * T + b * 128
                rs = rc[0] % NXR
                rc[0] += 1
                P.dma("sp", xres[rs][:], src[s, t0:t0 + 128, o * 512:(o + 1) * 512], [], [xres_b[rs]], xres_sem[rs])
                P.op("pool", lambda e, rs=rs, o=o: e.tensor_tensor(out=xres[rs][:], in0=xres[rs][:],
                                                                   in1=bo[:, o * 512:(o + 1) * 512], op=ALU.add),
                     [xres_b[rs], bo_b], [xres_b[rs]], n=512)
                return rs

            rsl = [ld(0), ld(1), ld(2)]
            for gi, (b, o) in enumerate(grp):
                t0 = i * T + b * 128
                rs = rsl[gi]
                q = gi % 4
                for ch in range(NCH):
                    P.op("pe", lambda e, ch=ch, b=b, o=o, q=q: e.matmul(
                        B[q][:], lhsT=sT[:, ch, b * 128:(b + 1) * 128], rhs=Wout[:, ch, o * 512:(o + 1) * 512],
                        start=(ch == 0), stop=(ch == NCH - 1)), [sT_b[ch], Wout_b], [B_b[q]])
                P.op("dve", lambda e, rs=rs, q=q: e.tensor_tensor(out=xres[rs][:], in0=B[q][:], in1=xres[rs][:], op=ALU.add),
                     [B_b[q], xres_b[rs]], [xres_b[rs]])
                P.dma("sp", dst[s, t0:t0 + 128, o * 512:(o + 1) * 512], xres[rs][:], [xres_b[rs]], [], xres_sem[rs])
                if gi + 3 < len(grp):
                    rsl.append(ld(gi + 3))

        norm_b(norm_a(0))
        in_head(0)
        for ch in range(NCH):
            in_pair(0, ch)
        build_dg()
        for ti in range(len(tiles)):
            nxt = ti + 1 < len(tiles)
            if nxt:
                slots = norm_a(ti + 1)
            stage_conv(ti)
            stage_stats(ti)
            if nxt:
                norm_b(slots)
                in_head(ti + 1)
            for ch in range(NCH):
                if nxt:
                    in_pair(ti + 1, ch)
                ln_chunk(ti, ch)
            stage_out(ti)
        P.barrier()


def fox_phase(P, c, li, src, dst):
    nc = P.nc
    j = li // 3
    S = c.S
    NT = S // T
    NKB = S // 128
    tiles = tile_list(c)
    w_in = c.w["fox_w_in"][j]
    with ExitStack() as st:
        N = alloc_norm(P, st, c, c.w["norm_mix"][li:li + 1, :], n_xnb=4)
        hT2 = P.sbuf(st, "hT2", [128, NCH, T], BF16)
        hTs = [N.hT, hT2]
        hTs_b = [N.hT_b, [Buf("hT2_%d" % i) for i in range(NB)]]
        WC = 3 * D + NH
        Wi = [P.sbuf(st, "wi%d" % k, [128, WC], BF16) for k in range(NCH)]
        Wi_b = [Buf("wi%d" % k) for k in range(NCH)]
        for k in range(NCH):
            P.dma("pool", Wi[k][:], w_in[k * 128:(k + 1) * 128, :], [], [Wi_b[k]], P.new_sem(), max_dma_last_dim=4096)
        gcol = P.sbuf(st, "gcol", [128, 2], F32)
        gcol_b = Buf("gcol")
        gsem = P.new_sem()
        for half in range(2):
            P.dma("sp", gcol[half * 64:(half + 1) * 64, 0:1], c.w["fox_q_gain"][j].rearrange("(h o) -> h o", o=1), [], [gcol_b], gsem)
            P.dma("sp", gcol[half * 64:(half + 1) * 64, 1:2], c.w["fox_k_gain"][j].rearrange("(h o) -> h o", o=1), [], [gcol_b], gsem)
        P.op("pool", lambda e: e.tensor_scalar(out=gcol[:, 0:1], in0=gcol[:, 0:1], scalar1=0.125, scalar2=1.0,
                                               op0=ALU.mult, op1=ALU.mult), [gcol_b], [gcol_b], n=1)
        bfc = P.sbuf(st, "bfc", [NH, 1], F32)
        bfc_b = Buf("bfc")
        P.dma("sp", bfc[:, 0:1], c.w["fox_b_f"][j].rearrange("(h o) -> h o", o=1), [], [bfc_b], P.new_sem())
        P.op("pool", lambda e: e.tensor_scalar(out=bfc[:], in0=bfc[:], scalar1=-1.0, scalar2=1.0, op0=ALU.mult, op1=ALU.mult),
             [bfc_b], [bfc_b], n=1)
        blk = P.sbuf(st, "blk", [128, 128], BF16)
        blk_b = Buf("blk")
        P.op("pool", lambda e: e.memset(blk[:], 0.0), [], [blk_b], n=128)
        P.op("pool", lambda e: e.memset(blk[0:64, 0:64], 1.0 / HD), [blk_b], [blk_b], n=64)
        P.op("pool", lambda e: e.memset(blk[64:128, 64:128], 1.0 / HD), [blk_b], [blk_b], n=64)
        ones16 = P.sbuf(st, "ones16", [NH, T], F32)
        ones16_b = Buf("ones16")
        P.op("pool", lambda e: e.memset(ones16[:], 1.0), [], [ones16_b], n=T)
        pq = [P.psum(st, "pq%d" % i, [128, T], F32) for i in range(3)]
        pq_b = [Buf("pq") for i in range(3)]
        pm = [P.psum(st, "pm%d" % i, [128, T], F32) for i in range(2)]
        pm_b = [Buf("pm") for i in range(2)]
        pv = [P.psum(st, "pv%d" % i, [128, T], F32) for i in range(2)]
        pv_b = [Buf("pv") for i in range(2)]
        sqh = [P.sbuf(st, "sqh%d" % i, [128, T], BF16) for i in range(2)]
        sqh_b = [Buf("sqh") for i in range(2)]
        ln1 = [P.sbuf(st, "ln1%d" % i, [128, T], F32) for i in range(2)]
        ln1_b = [Buf("ln1") for i in range(2)]
        rs = [P.sbuf(st, "rs%d" % i, [128, T], F32) for i in range(2)]
        rs_b = [Buf("rs") for i in range(2)]
        qn = [P.sbuf(st, "qn%d" % i, [128, T], BF16) for i in range(4)]
        qn_b = [Buf("qn") for i in range(4)]
        qn_sem = [P.new_sem() for _ in range(4)]
        vb = [P.sbuf(st, "vb%d" % i, [128, D], BF16) for i in range(2)]
        vb_b = [Buf("vb") for i in range(2)]
        vb_sem = [P.new_sem() for _ in range(2)]
        e1 = P.sbuf(st, "e1", [NH, T], F32)
        e1_b = Buf("e1")
        cc_t = [P.sbuf(st, "cc%d" % i, [NH, T], F32) for i in range(2)]
        cc_b = [Buf("cc") for i in range(2)]
        r1 = P.sbuf(st, "r1", [NH, T], F32)
        r1_b = Buf("r1")
        cq = [P.sbuf(st, "cq%d" % i, [NH, 3, T], BF16) for i in range(2)]
        ck = [P.sbuf(st, "ck%d" % i, [NH, 3, T], BF16) for i in range(2)]
        cq_b = [Buf("cq") for i in range(2)]
        cq_sem = [P.new_sem() for _ in range(2)]
        cnt = {"qn": 0, "vb": 0}
        def f1_norm_a(ti):
            s, i = tiles[ti]
            return [norm_part_a(P, c, N, src[s, i * T + b * 128:i * T + (b + 1) * 128, :]) for b in range(NB)]

        def f1_norm_b(ti, slots, b):
            hs_ = ti % 2
            norm_part_b(P, c, N, slots[b], b, out_ap=hTs[hs_][:, :, b * 128:(b + 1) * 128], out_bufs=[hTs_b[hs_][b]])

        sl0 = f1_norm_a(0)
        for b in range(NB):
            f1_norm_b(0, sl0, b)
        for ti, (s, i) in enumerate(tiles):
            t0 = i * T
            hT = hTs[ti % 2]
            hT_b = hTs_b[ti % 2]
            nxt = ti + 1 < len(tiles)
            if nxt:
                nslots = f1_norm_a(ti + 1)
            def proj(cc):
                sl = cc % 3
                for k in range(NCH):
                    P.op("pe", lambda e, sl=sl, k=k, cc=cc, hT=hT: e.matmul(
                        pq[sl][:], lhsT=Wi[k][:, cc * 128:(cc + 1) * 128], rhs=hT[:, k, :],
                        start=(k == 0), stop=(k == NCH - 1)), [Wi_b[k]] + hT_b, [pq_b[sl]])
                s2 = cc % 2
                P.op("act", lambda e, sl=sl, s2=s2: e.activation(out=sqh[s2][:], in_=pq[sl][:], func=AF.Square),
                     [pq_b[sl]], [sqh_b[s2]])

            def finish(cc):
                sl = cc % 3
                s2 = cc % 2
                P.op("pe", lambda e, s2=s2: e.matmul(pm[s2][:], lhsT=blk[:], rhs=sqh[s2][:], start=True, stop=True),
                     [blk_b, sqh_b[s2]], [pm_b[s2]])
                P.op("act", lambda e, s2=s2: e.activation(out=ln1[s2][:], in_=pm[s2][:], func=AF.Ln, bias=c.eps_col[:, 0:1],
                                                          scale=1.0), [pm_b[s2], c.const_b], [ln1_b[s2]])
                P.op("act", lambda e, s2=s2: e.activation(out=rs[s2][:], in_=ln1[s2][:], func=AF.Exp, scale=-0.5),
                     [ln1_b[s2]], [rs_b[s2]])
                qs = cnt["qn"] % 4
                cnt["qn"] += 1
                gi = 0 if cc < NCH else 1
                P.op("dve", lambda e, sl=sl, s2=s2, qs=qs, gi=gi: e.scalar_tensor_tensor(
                    out=qn[qs][:], in0=pq[sl][:], scalar=gcol[:, gi:gi + 1], in1=rs[s2][:], op0=ALU.mult, op1=ALU.mult),
                    [pq_b[sl], rs_b[s2], gcol_b], [qn_b[qs]])
                dd = c.qT_d if cc < NCH else c.kT_d
                r0 = (cc % NCH) * 128
                P.dma("sp", dd[s, r0:r0 + 128, t0:t0 + T], qn[qs][:], [qn_b[qs]], [], qn_sem[qs])

            proj(0)
            for cc in range(2 * NCH):
                if cc + 1 < 2 * NCH:
                    proj(cc + 1)
                finish(cc)
                if nxt and cc % 4 == 3:
                    f1_norm_b(ti + 1, nslots, cc // 4)
            P.op_dummy = None
            for k in range(NCH):
                P.op("pe", lambda e, k=k, hT=hT: e.matmul(pv[0][0:NH, :], lhsT=Wi[k][:, 3 * D:3 * D + NH], rhs=hT[:, k, :],
                                                   start=(k == 0), stop=(k == NCH - 1)), [Wi_b[k]] + hT_b, [pv_b[0]])
            P.op("act", lambda e: e.activation(out=e1[:], in_=pv[0][0:NH, :], func=AF.Exp, scale=-1.0, bias=bfc[:, 0:1]),
                 [pv_b[0], bfc_b], [e1_b])
            P.op("act", lambda e: e.activation(out=e1[:], in_=e1[:], func=AF.Ln, bias=1.0, scale=1.0), [e1_b], [e1_b])
            cs = ti % 2
            if i == 0:
                P.op("dve", lambda e, cs=cs: e.tensor_tensor_scan(out=cc_t[cs][:], data0=ones16[:], data1=e1[:], initial=0.0,
                                                                   op0=ALU.mult, op1=ALU.subtract),
                     [ones16_b, e1_b], [cc_b[cs]])
            else:
                P.op("dve", lambda e, cs=cs: e.tensor_tensor_scan(out=cc_t[cs][:], data0=ones16[:], data1=e1[:],
                                                                   initial=cc_t[1 - cs][:, T - 1:T], op0=ALU.mult,
                                                                   op1=ALU.subtract),
                     [ones16_b, e1_b, cc_b[1 - cs]], [cc_b[cs]])
            P.op("pool", lambda e, cs=cs: e.tensor_copy(out=cq[cs][:, 0, :], in_=cc_t[cs][:]), [cc_b[cs]], [cq_b[cs]])
            P.op("pool", lambda e, cs=cs: e.tensor_tensor(out=r1[:], in0=cc_t[cs][:], in1=cq[cs][:, 0, :], op=ALU.subtract),
                 [cc_b[cs], cq_b[cs]], [r1_b])
            P.op("pool", lambda e, cs=cs: e.tensor_copy(out=cq[cs][:, 1, :], in_=r1[:]), [r1_b], [cq_b[cs]])
            P.op("pool", lambda e, cs=cs: e.tensor_tensor(out=r1[:], in0=r1[:], in1=cq[cs][:, 1, :], op=ALU.subtract),
                 [r1_b, cq_b[cs]], [r1_b])
            P.op("pool", lambda e, cs=cs: e.tensor_copy(out=cq[cs][:, 2, :], in_=r1[:]), [r1_b], [cq_b[cs]])
            P.op("pool", lambda e, cs=cs: e.tensor_scalar(out=ck[cs][:], in0=cq[cs][:], scalar1=-1.0, scalar2=1.0,
                                                          op0=ALU.mult, op1=ALU.mult), [cq_b[cs]], [cq_b[cs]], n=3 * T)
            P.dma("sp", c.cq_d[s, :, :, t0:t0 + T], cq[cs][:], [cq_b[cs]], [], cq_sem[cs])
            P.dma("sp", c.ck_d[s, :, :, t0:t0 + T], ck[cs][:], [cq_b[cs]], [], cq_sem[cs])
            for b in range(NB):
                vs = cnt["vb"] % 2
                cnt["vb"] += 1
                for o in range(2):
                    for k in range(NCH):
                        P.op("pe", lambda e, k=k, b=b, o=o, hT=hT: e.matmul(
                            pv[1][:], lhsT=hT[:, k, b * 128:(b + 1) * 128],
                            rhs=Wi[k][:, 2 * D + o * 512:2 * D + (o + 1) * 512], start=(k == 0), stop=(k == NCH - 1)),
                            [Wi_b[k], hT_b[b]], [pv_b[1]])
                    P.op("act", lambda e, vs=vs, o=o: e.copy(out=vb[vs][:, o * 512:(o + 1) * 512], in_=pv[1][:]),
                         [pv_b[1]], [vb_b[vs]])
                P.dma("sp", c.v_d[s, t0 + b * 128:t0 + (b + 1) * 128, :], vb[vs][:], [vb_b[vs]], [], vb_sem[vs])
        P.barrier()
    with ExitStack() as st:
        KR = HD + 6
        QA = [P.sbuf(st, "QA%d" % i, [KR, S], BF16) for i in range(2)]
        KA = [P.sbuf(st, "KA%d" % i, [KR, S], BF16) for i in range(2)]
        VA = [P.sbuf(st, "VA%d" % i, [128, NKB, 128], BF16) for i in range(2)]
        hd_b = [Buf("head%d" % i) for i in range(2)]
        hd_sem = [P.new_sem() for _ in range(2)]
        for i in range(2):
            P.op("pool", lambda e, i=i: e.memset(QA[i][HD:KR, :], 1.0), [], [hd_b[i]], n=S)
            P.op("pool", lambda e, i=i: e.memset(KA[i][HD:KR, :], 1.0), [], [hd_b[i]], n=S)
            P.op("pool", lambda e, i=i: e.memset(VA[i][:, :, HD:128], 1.0), [], [hd_b[i]], n=NKB * 64)
        NPS = 4
        ps = [P.psum(st, "ps%d" % i, [128, T], F32) for i in range(NPS)]
        ps_b = [Buf("ps") for i in range(NPS)]
        po = [P.psum(st, "po%d" % i, [128, T], F32) for i in range(2)]
        po_b = [Buf("po") for i in range(2)]
        pt = [P.sbuf(st, "pt%d" % i, [128, T], BF16) for i in range(NPS)]
        pt_b = [Buf("pt") for i in range(NPS)]
        rec = [P.sbuf(st, "rec%d" % i, [HD, T], F32) for i in range(2)]
        rec_b = [Buf("rec") for i in range(2)]
        ob = [P.sbuf(st, "ob%d" % i, [HD, T], BF16) for i in range(2)]
        ob_b = [Buf("ob") for i in range(2)]
        ob_sem = [P.new_sem() for _ in range(2)]
        heads = [(s, h) for s in range(c.n_seq) for h in range(NH)]

        def load_head(n):
            s, h = heads[n]
            sl = n % 2
            P.dma("sp", QA[sl][0:HD, :], c.qT_d[s, h * HD:(h + 1) * HD, :], [], [hd_b[sl]], hd_sem[sl])
            P.dma("sp", QA[sl][HD:HD + 3, :], c.cq_d[s, h, :, :], [], [hd_b[sl]], hd_sem[sl])
            P.dma("sp", KA[sl][0:HD, :], c.kT_d[s, h * HD:(h + 1) * HD, :], [], [hd_b[sl]], hd_sem[sl])
            P.dma("sp", KA[sl][HD + 3:HD + 6, :], c.ck_d[s, h, :, :], [], [hd_b[sl]], hd_sem[sl])
            vsrc = c.v_d[s].rearrange("(kb p) f -> p kb f", p=128)
            nsp = max(1, NKB // 8)
            for q in range(nsp):
                k0, k1 = q * NKB // nsp, (q + 1) * NKB // nsp
                P.dma("sp", VA[sl][:, k0:k1, 0:HD], vsrc[:, k0:k1, h * HD:(h + 1) * HD], [], [hd_b[sl]], hd_sem[sl])

        load_head(0)
        LA = 3
        ocnt = 0
        for n, (s, h) in enumerate(heads):
            sl = n % 2
            if n + 1 < len(heads):
                load_head(n + 1)
            items = []
            for qt in range(NT):
                nkb = 4 * qt + 4
                for kb in range(nkb):
                    items.append((qt, kb, kb == 0, kb == nkb - 1))

            def emit_s(ii, sl=sl):
                qt, kb, first, last = items[ii]
                r = kb - 4 * qt
                c0 = max(r, 0) * 128
                p = ii % NPS
                P.op("pe", lambda e, p=p, kb=kb, qt=qt, c0=c0, sl=sl: e.matmul(
                    ps[p][:, c0:T], lhsT=KA[sl][:, kb * 128:(kb + 1) * 128], rhs=QA[sl][:, qt * T + c0:(qt + 1) * T],
                    start=True, stop=True), [hd_b[sl]], [ps_b[p]])

            for ii in range(min(LA, len(items))):
                emit_s(ii)
            for ii, (qt, kb, first, last) in enumerate(items):
                r = kb - 4 * qt
                c0 = max(r, 0) * 128
                p = ii % NPS
                oq = (n * NT + qt) % 2
                P.op("act", lambda e, p=p, c0=c0: e.activation(out=pt[p][:, c0:T], in_=ps[p][:, c0:T], func=AF.Exp),
                     [ps_b[p]], [pt_b[p]], n=T - c0)
                if r >= 0:
                    P.op("pool", lambda e, p=p, c0=c0: e.affine_select(
                        out=pt[p][:, c0:c0 + 128], in_=pt[p][:, c0:c0 + 128], pattern=[[1, 128]], compare_op=ALU.is_ge,
                        fill=0.0, base=0, channel_multiplier=-1), [pt_b[p]], [pt_b[p]], n=128)
                P.op("pe", lambda e, p=p, c0=c0, kb=kb, oq=oq, first=first, last=last, sl=sl: e.matmul(
                    po[oq][:, c0:T], lhsT=VA[sl][:, kb, :], rhs=pt[p][:, c0:T], start=first, stop=last),
                    [hd_b[sl], pt_b[p]], [po_b[oq]])
                if ii + LA < len(items):
                    emit_s(ii + LA)
                if last:
                    os_ = ocnt % 2
                    ocnt += 1
                    P.op("dve", lambda e, oq=oq, os_=os_: e.reciprocal(out=rec[os_][:], in_=po[oq][HD:128, :]),
                         [po_b[oq]], [rec_b[os_]])
                    P.op("dve", lambda e, oq=oq, os_=os_: e.tensor_tensor(out=ob[os_][:], in0=po[oq][0:HD, :], in1=rec[os_][:],
                                                                          op=ALU.mult), [po_b[oq], rec_b[os_]], [ob_b[os_]])
                    P.dma("sp", c.o_d[s, h * HD:(h + 1) * HD, qt * T:(qt + 1) * T], ob[os_][:], [ob_b[os_]], [], ob_sem[os_])
        P.barrier()
    with ExitStack() as st:
        Wo = P.sbuf(st, "wo", [128, NCH, D], BF16)
        Wo_b = Buf("wo")
        P.dma("pool", Wo[:], c.w["fox_w_o"][j].rearrange("(k p) d -> p k d", p=128), [], [Wo_b], P.new_sem(),
              max_dma_last_dim=4096)
        ot = [P.sbuf(st, "ot%d" % i, [128, NCH, T], BF16) for i in range(2)]
        ot_b = [Buf("ot") for i in range(2)]
        ot_sem = [P.new_sem() for _ in range(2)]
        pd = [P.psum(st, "pd%d" % i, [128, T], F32) for i in range(4)]
        pd_b = [Buf("pd") for i in range(4)]
        xres = [P.sbuf(st, "xres%d" % i, [128, D], F32) for i in range(3)]
        xres_b = [Buf("xres%d" % i) for i in range(3)]
        xres_sem = [P.new_sem() for _ in range(3)]
        rcnt = 0

        def load_ot(ti):
            s, i = tiles[ti]
            P.dma("sp", ot[ti % 2][:], c.o_d[s].rearrange("(k p) t -> p k t", p=128)[:, :, i * T:(i + 1) * T], [],
                  [ot_b[ti % 2]], ot_sem[ti % 2])

        load_ot(0)
        for ti, (s, i) in enumerate(tiles):
            if ti + 1 < len(tiles):
                load_ot(ti + 1)
            osl = ti % 2
            for b in range(NB):
                t0 = i * T + b * 128
                rs_ = rcnt % 3
                rcnt += 1
                P.dma("sp", xres[rs_][:], src[s, t0:t0 + 128, :], [], [xres_b[rs_]], xres_sem[rs_])
                for o in range(2):
                    q = (2 * b + o) % 4
                    for k in range(NCH):
                        P.op("pe", lambda e, k=k, b=b, o=o, q=q, osl=osl: e.matmul(
                            pd[q][:], lhsT=ot[osl][:, k, b * 128:(b + 1) * 128], rhs=Wo[:, k, o * 512:(o + 1) * 512],
                            start=(k == 0), stop=(k == NCH - 1)), [ot_b[osl], Wo_b], [pd_b[q]])
                    P.op("dve", lambda e, rs_=rs_, o=o, q=q: e.tensor_tensor(
                        out=xres[rs_][:, o * 512:(o + 1) * 512], in0=pd[q][:], in1=xres[rs_][:, o * 512:(o + 1) * 512],
                        op=ALU.add), [pd_b[q], xres_b[rs_]], [xres_b[rs_]])
                P.dma("act", dst[s, t0:t0 + 128, :], xres[rs_][:], [xres_b[rs_]], [], xres_sem[rs_])
        P.barrier()


WEIGHT_NAMES = ["norm_mix", "norm_ffn", "conv_w_in", "conv_b_in", "conv_dw", "conv_dw_b", "conv_ln_g", "conv_ln_b",
                "conv_w_out", "conv_b_out", "pool_w", "pool_b", "pool_scale", "fox_w_in", "fox_b_f", "fox_q_gain",
                "fox_k_gain", "fox_w_o", "ffn_w_up", "ffn_dw", "ffn_dw_b", "ffn_w_down"]


def build_program(n_seq, S, phases, wshapes):
    nc = bass.Bass("TRN2", target_bir_lowering=False)
    c = Ctx()
    c.n_seq, c.S = n_seq, S
    x = nc.dram_tensor("x", [n_seq, S, D], F32, kind="ExternalInput").ap()
    y = nc.dram_tensor("y", [n_seq, S, D], F32, kind="ExternalOutput").ap()
    c.w = {}
    for name in WEIGHT_NAMES:
        c.w[name] = nc.dram_tensor(name, list(wshapes[name]), F32, kind="ExternalInput").ap()
    if any(k == "fox" for k, _ in phases):
        c.qT_d = nc.dram_tensor("qT_d", [n_seq, D, S], BF16, kind="Internal").ap()
        c.kT_d = nc.dram_tensor("kT_d", [n_seq, D, S], BF16, kind="Internal").ap()
        c.v_d = nc.dram_tensor("v_d", [n_seq, S, D], BF16, kind="Internal").ap()
        c.o_d = nc.dram_tensor("o_d", [n_seq, D, S], BF16, kind="Internal").ap()
        c.cq_d = nc.dram_tensor("cq_d", [n_seq, NH, 3, S], BF16, kind="Internal").ap()
        c.ck_d = nc.dram_tensor("ck_d", [n_seq, NH, 3, S], BF16, kind="Internal").ap()
    with ExitStack() as st:
        P = Prog(nc, st)
        c.idf, c.idb, c.const_b = make_identity(P, st)
        c.nhalf = P.sbuf(st, "nhalf", [128, 1], F32)
        P.op("pool", lambda e: e.memset(c.nhalf[:], -0.5), [], [c.const_b], n=1)
        c.eps_col = P.sbuf(st, "eps_col", [128, 1], F32)
        P.op("pool", lambda e: e.memset(c.eps_col[:], EPS), [], [c.const_b], n=1)
        src = x
        for kind, li in phases:
            if kind == "ffn":
                ffn_phase(P, c, li, src, y)
            elif kind == "pool":
                pool_phase(P, c, li, src, y)
            elif kind == "conv":
                conv_phase(P, c, li, src, y)
            elif kind == "fox":
                fox_phase(P, c, li, src, y)
            else:
                raise ValueError(kind)
            src = y
        P.barrier()
        P.emit()
    return nc


ALL_PHASES = [("conv", 0), ("ffn", 0), ("pool", 1), ("ffn", 1), ("fox", 2), ("ffn", 2), ("conv", 3), ("ffn", 3)]
N_CORES = 8


def kernel(**inputs):
    x = np.ascontiguousarray(np.asarray(inputs["x"], dtype=np.float32))
    Bt, S, Dm = x.shape
    n_seq = Bt // N_CORES
    w = {k: np.ascontiguousarray(np.asarray(inputs[k], dtype=np.float32)) for k in WEIGHT_NAMES}
    nc = build_program(n_seq, S, ALL_PHASES, {k: v.shape for k, v in w.items()})
    in_maps = []
    for i in range(N_CORES):
        m = {"x": x[i * n_seq:(i + 1) * n_seq]}
        m.update(w)
        in_maps.append(m)
    res = run_bass_kernel_spmd(nc, in_maps, core_ids=list(range(N_CORES)))
    return np.concatenate([r["y"] for r in res.results], axis=0)
```

```python
import os
import numpy as np
from contextlib import ExitStack
import concourse.bass as bass
import concourse.mybir as mybir
from concourse.bass_utils import run_bass_kernel_spmd

F32 = mybir.dt.float32
BF16 = mybir.dt.bfloat16
ALU = mybir.AluOpType
AF = mybir.ActivationFunctionType

D = 1024
NCH = 8
DFF = 2816
NF = 22
T = 512
NB = 4
EPS = 1e-6
CONVW = 31
NH = 16
HD = 64


class Buf:
    __slots__ = ("name", "w", "rs")

    def __init__(self, name):
        self.name = name
        self.w = None
        self.rs = {}


class Op:
    __slots__ = ("eng", "fn", "deps", "dwaits", "long", "dma_sem", "need_inc", "val")

    def __init__(self, eng, fn, long):
        self.eng = eng
        self.fn = fn
        self.long = long
        self.deps = {}
        self.dwaits = {}
        self.dma_sem = None
        self.need_inc = False
        self.val = 0


class Prog:
    CENG = ("pe", "act", "dve", "pool")
    ENG = ("pe", "act", "dve", "pool", "sp")

    def __init__(self, nc, stack, n_dma_sems=64):
        self.nc = nc
        self.stack = stack
        self.ops = {e: [] for e in self.ENG}
        self.esem = {e: stack.enter_context(nc.semaphore("s_" + e)) for e in self.CENG}
        self.dsem = [stack.enter_context(nc.semaphore("d%d" % i)) for i in range(n_dma_sems)]
        self.dcnt = [0] * n_dma_sems
        self.dnext = 0
        self.last = {}
        self.n_ops = 0
        self.uid = 0

    def sbuf(self, st, name, shape, dtype):
        self.uid += 1
        return st.enter_context(self.nc.sbuf_tensor("%s_%d" % (name, self.uid), list(shape), dtype))

    def psum(self, st, name, shape, dtype):
        self.uid += 1
        return st.enter_context(self.nc.psum_tensor("%s_%d" % (name, self.uid), list(shape), dtype))

    def new_sem(self):
        i = self.dnext
        self.dnext = (self.dnext + 1) % len(self.dsem)
        return i

    def _record(self, o, reads, writes):
        deps = o.deps
        dw = o.dwaits
        for b in reads:
            w = b.w
            if w is not None:
                if w.dma_sem is not None:
                    dw[w.dma_sem] = self.dcnt[w.dma_sem]
                else:
                    deps[w] = True
        for b in writes:
            w = b.w
            if w is not None:
                if w.dma_sem is not None:
                    dw[w.dma_sem] = self.dcnt[w.dma_sem]
                elif w not in deps:
                    deps[w] = False
            for r in b.rs.values():
                if r.dma_sem is not None:
                    dw[r.dma_sem] = self.dcnt[r.dma_sem]
                elif r not in deps:
                    deps[r] = False
        deps.pop(o, None)
        for b in writes:
            b.w = o
            b.rs = {}
        key = o.eng if o.dma_sem is None else ("d", o.dma_sem)
        for b in reads:
            if b.w is not o:
                b.rs[key] = o
        self.ops[o.eng].append(o)
        self.last[key] = o
        self.n_ops += 1
        return o

    def op(self, eng, fn, reads=(), writes=(), n=512):
        return self._record(Op(eng, fn, n >= 256), reads, writes)

    def dma(self, queue, out, in_, reads, writes, sem, **kw):
        o = Op(queue, lambda e: e.dma_start(out=out, in_=in_, **kw), True)
        o.dma_sem = sem
        self._record(o, reads, writes)
        self.dcnt[sem] += 16
        return o

    def barrier(self):
        lasts = list(self.last.values())
        for e in self.ENG:
            o = Op(e, None, True)
            for l in lasts:
                if l.dma_sem is not None:
                    o.dwaits[l.dma_sem] = self.dcnt[l.dma_sem]
                elif l.eng != e:
                    o.deps[l] = False
            self.ops[e].append(o)

    def emit(self):
        nc = self.nc
        for e in self.ENG:
            for o in self.ops[e]:
                for d, raw in o.deps.items():
                    if d.eng != o.eng:
                        d.need_inc = True
                    elif o.eng != "pe":
                        d.need_inc = True
        for e in self.CENG:
            c = 0
            for o in self.ops[e]:
                if o.need_inc:
                    c += 1
                    o.val = c
        esem, dsem = self.esem, self.dsem

        def run(e, eng):
            waited_e = {x: 0 for x in self.CENG}
            waited_d = {}
            for o in self.ops[e]:
                for d, raw in o.deps.items():
                    if d.eng == e and e == "pe":
                        continue
                    if waited_e[d.eng] < d.val:
                        eng.wait_ge(esem[d.eng], d.val)
                        waited_e[d.eng] = d.val
                for s, v in o.dwaits.items():
                    if waited_d.get(s, 0) < v:
                        eng.wait_ge(dsem[s], v)
                        waited_d[s] = v
                if o.fn is None:
                    continue
                ins = o.fn(eng)
                if o.dma_sem is not None:
                    ins.then_inc(dsem[o.dma_sem], 16)
                elif o.need_inc:
                    ins.then_inc(esem[e], 1)

        with nc.Block() as block:
            @block.tensor
            def _(eng):
                run("pe", eng)

            @block.scalar
            def _(eng):
                run("act", eng)

            @block.vector
            def _(eng):
                run("dve", eng)

            @block.gpsimd
            def _(eng):
                run("pool", eng)

            @block.sync
            def _(eng):
                run("sp", eng)


class Ctx:
    pass


def load_cols(P, st, name, rows_aps, pe_bank, pe_bank_buf, c):
    ident_f = c.idf
    nc = P.nc
    R = sum(a.shape[0] for a in rows_aps)
    cols = P.sbuf(st, name, [128, R], F32)
    cb = Buf(name)
    done = 0
    grp = []
    stage_i = 0
    pending = []
    cur = 0
    for a in rows_aps:
        r0 = 0
        while r0 < a.shape[0]:
            take = min(a.shape[0] - r0, 128 - cur)
            pending.append((cur, a[r0:r0 + take, :], take))
            cur += take
            r0 += take
            if cur == 128:
                grp.append(pending)
                pending = []
                cur = 0
    if pending:
        grp.append(pending)
    for gi, pend in enumerate(grp):
        nrows = sum(t for _, _, t in pend)
        stg = P.sbuf(st, "%s_stg%d" % (name, gi), [128, 128], F32)
        sb = Buf("stg")
        sem = P.new_sem()
        for (p0, ap, take) in pend:
            P.dma("sp", stg[p0:p0 + take, :], ap, [], [sb], sem)
        P.op("pe", lambda e, stg=stg, nrows=nrows: e.transpose(out=pe_bank[:, 0:nrows], in_=stg[0:nrows, :], identity=ident_f[0:nrows, 0:nrows]),
             [sb, c.const_b], [pe_bank_buf], n=128)
        P.op("dve", lambda e, done=done, nrows=nrows: e.tensor_copy(out=cols[:, done:done + nrows], in_=pe_bank[:, 0:nrows]),
             [pe_bank_buf], [cb], n=nrows)
        done += nrows
    return cols, cb


def make_identity(P, st):
    nc = P.nc
    idf = P.sbuf(st, "ident_f", [128, 128], F32)
    idb = P.sbuf(st, "ident_b", [128, 128], BF16)
    b = Buf("ident")
    P.op("pool", lambda e: e.memset(idf[:], 1.0), [], [b], n=128)
    P.op("pool", lambda e: e.affine_select(out=idf[:], in_=idf[:], pattern=[[-1, 128]], compare_op=ALU.is_equal,
                                           fill=0.0, base=0, channel_multiplier=1), [b], [b], n=128)
    P.op("pool", lambda e: e.tensor_copy(out=idb[:], in_=idf[:]), [b], [b], n=128)
    return idf, idb, b


def alloc_norm(P, st, c, g_dram_row, want_hT=True, n_xnb=2):
    N = Ctx()
    N.xin = [P.sbuf(st, "xin%d" % i, [128, D], F32) for i in range(2)]
    N.xin_b = [Buf("xin%d" % i) for i in range(2)]
    N.xin_sem = [P.new_sem() for _ in range(2)]
    N.xnb = [P.sbuf(st, "xnb%d" % i, [128, D], BF16) for i in range(n_xnb)]
    N.xnb_b = [Buf("xnb%d" % i) for i in range(n_xnb)]
    N.n_xnb = n_xnb
    N.cnt_a = 0
    N.ss = [P.sbuf(st, "ss%d" % i, [128, 4], F32) for i in range(2)]
    N.ss_b = [Buf("ss%d" % i) for i in range(2)]
    N.gbc = P.sbuf(st, "gbc", [128, D], F32)
    N.gbc_b = Buf("gbc")
    P.dma("sp", N.gbc[:], g_dram_row.partition_broadcast(128), [], [N.gbc_b], P.new_sem())
    if want_hT:
        N.hT = P.sbuf(st, "hT", [128, NCH, T], BF16)
        N.hT_b = [Buf("hT%d" % i) for i in range(NB)]
    if want_hT:
        N.tp = P.psum(st, "tp", [128, D], BF16)
        N.tp_b = Buf("tp")
    N.cnt = 0
    return N


def norm_block(P, c, N, src_ap, b, out_ap=None, out_bufs=None):
    sx = norm_part_a(P, c, N, src_ap)
    norm_part_b(P, c, N, sx, b, out_ap, out_bufs)


def tile_list(c):
    return [(s, i) for s in range(c.n_seq) for i in range(c.S // T)]


def norm_part_a(P, c, N, src_ap):
    s = N.cnt % 2
    N.cnt += 1
    sx = N.cnt_a % N.n_xnb
    N.cnt_a += 1
    xin, xnb, ss = N.xin[s], N.xnb[sx], N.ss[s]
    xnb_b = N.xnb_b[sx]
    P.dma("sp", xin[:], src_ap, [], [N.xin_b[s]], N.xin_sem[s])
    P.op("act", lambda e: e.activation(out=xnb[:], in_=xin[:], func=AF.Square, accum_out=ss[:, 0:1]),
         [N.xin_b[s]], [xnb_b, N.ss_b[s]], n=D)
    P.op("pool", lambda e: e.tensor_scalar(out=ss[:, 1:2], in0=ss[:, 0:1], scalar1=1.0 / D, scalar2=EPS,
                                           op0=ALU.mult, op1=ALU.add), [N.ss_b[s]], [N.ss_b[s]], n=1)
    P.op("pool", lambda e: e.tensor_tensor(out=ss[:, 2:3], in0=ss[:, 1:2], in1=c.nhalf[:, 0:1], op=ALU.pow),
         [N.ss_b[s], c.const_b], [N.ss_b[s]], n=1)
    P.op("dve", lambda e: e.scalar_tensor_tensor(out=xnb[:], in0=xin[:], scalar=ss[:, 2:3], in1=N.gbc[:],
                                                 op0=ALU.mult, op1=ALU.mult),
         [N.xin_b[s], N.ss_b[s], N.gbc_b], [xnb_b], n=D)
    return sx


def norm_part_b(P, c, N, s, b, out_ap=None, out_bufs=None):
    xnb = N.xnb[s]
    for k in range(NCH):
        P.op("pe", lambda e, k=k: e.transpose(out=N.tp[:, k * 128:(k + 1) * 128], in_=xnb[:, k * 128:(k + 1) * 128],
                                              identity=c.idb[:]),
             [N.xnb_b[s], c.const_b], [N.tp_b], n=128)
    if out_ap is None:
        out_ap = N.hT[:, :, b * 128:(b + 1) * 128]
        out_bufs = [N.hT_b[b]]
    P.op("act", lambda e: e.copy(out=out_ap, in_=N.tp[:].rearrange("p (k t) -> p k t", k=NCH)),
         [N.tp_b], out_bufs, n=D)


def ffn_phase(P, c, li, src, dst):
    nc = P.nc
    with ExitStack() as st:
        N = alloc_norm(P, st, c, c.w["norm_ffn"][li:li + 1, :])
        Wup = [P.sbuf(st, "wup%d" % k, [128, 2 * DFF], BF16) for k in range(NCH)]
        Wup_b = [Buf("wup%d" % k) for k in range(NCH)]
        for k in range(NCH):
            P.dma("pool", Wup[k][:], c.w["ffn_w_up"][li, k * 128:(k + 1) * 128, :], [], [Wup_b[k]], P.new_sem(),
                  max_dma_last_dim=4096)
        Wdn = P.sbuf(st, "wdn", [128, NF, D], BF16)
        Wdn_b = [Buf("wdn0"), Buf("wdn1")]
        wd_view = c.w["ffn_w_down"][li].rearrange("(j p) d -> p j d", p=128)
        P.dma("pool", Wdn[:, 0:11, :], wd_view[:, 0:11, :], [], [Wdn_b[0]], P.new_sem(), max_dma_last_dim=4096)
        P.dma("pool", Wdn[:, 11:22, :], wd_view[:, 11:22, :], [], [Wdn_b[1]], P.new_sem(), max_dma_last_dim=4096)
        pu = [[P.psum(st, "pu%d%d" % (s, v), [128, T], F32) for v in range(2)] for s in range(2)]
        pu_b = [[Buf("pu") for v in range(2)] for s in range(2)]
        pd = [P.psum(st, "pd%d" % i, [128, T], F32) for i in range(3)]
        pd_b = [Buf("pd") for i in range(3)]
        cols, cols_b = load_cols(P, st, "fcols",
                                 [c.w["ffn_dw"][li].rearrange("k (c p) -> (k c) p", p=128),
                                  c.w["ffn_dw_b"][li].rearrange("(c p) -> c p", p=128)],
                                 pd[0], pd_b[0], c)
        NC2 = 2 * NF

        def wcol(k, ch):
            return cols[:, k * NC2 + ch:k * NC2 + ch + 1]

        def bcol(ch):
            return cols[:, 3 * NC2 + ch:3 * NC2 + ch + 1]

        cv = [P.sbuf(st, "cv%d" % s, [128, 2, T], F32) for s in range(2)]
        cv_b = [[Buf("cv"), Buf("cg")] for s in range(2)]
        hs = [P.sbuf(st, "hs%d" % i, [128, NC2, 2], F32) for i in range(2)]
        hs_b = [Buf("hs%d" % i) for i in range(2)]
        fix = [P.sbuf(st, "fix%d" % i, [128, NC2, 2], F32) for i in range(2)]
        fix_b = [Buf("fix%d" % i) for i in range(2)]
        ftmp = P.sbuf(st, "ftmp", [128, NC2], F32)
        ftmp_b = Buf("ftmp")
        gT = P.sbuf(st, "gT", [128, NF, T], BF16)
        gT_b = [Buf("gT%d" % j) for j in range(NF)]
        NXR = 3
        xres = [P.sbuf(st, "xres%d" % i, [128, D], F32) for i in range(NXR)]
        xres_b = [Buf("xres%d" % i) for i in range(NXR)]
        xres_sem = [P.new_sem() for _ in range(NXR)]
        tiles = tile_list(c)

        def blk_src(ti, b):
            s, i = tiles[ti]
            t0 = i * T + b * 128
            return src[s, t0:t0 + 128, :]

        def stageB(j):
            sl = j % 2
            P.op("act", lambda e: e.activation(out=cv[sl][:, 1, :], in_=cv[sl][:, 1, :], func=AF.Silu),
                 [cv_b[sl][1]], [cv_b[sl][1]])
            P.op("pool", lambda e: e.tensor_tensor(out=gT[:, j, :], in0=cv[sl][:, 1, :], in1=cv[sl][:, 0, :], op=ALU.mult),
                 [cv_b[sl][0], cv_b[sl][1]], [gT_b[j]])

        for b in range(NB):
            sa = norm_part_a(P, c, N, blk_src(0, b))
            norm_part_b(P, c, N, sa, b)
        rcnt = 0
        for ti, (s, i) in enumerate(tiles):
            par = ti % 2
            nxt = ti + 1 < len(tiles)
            pend = []
            for j in range(NF):
                sl = j % 2
                for v in range(2):
                    ch = v * NF + j
                    for k in range(NCH):
                        P.op("pe", lambda e, v=v, k=k, ch=ch, sl=sl: e.matmul(
                            pu[sl][v][:], lhsT=Wup[k][:, ch * 128:(ch + 1) * 128], rhs=N.hT[:, k, :],
                            start=(k == 0), stop=(k == NCH - 1)),
                            [Wup_b[k]] + N.hT_b, [pu_b[sl][v]])
                for v in range(2):
                    ch = v * NF + j
                    P.op("act", lambda e, v=v, sl=sl, ch=ch: e.activation(
                        out=cv[sl][:, v, :], in_=pu[sl][v][:], func=AF.Identity, scale=wcol(2, ch), bias=bcol(ch)),
                        [pu_b[sl][v], cols_b], [cv_b[sl][v]])
                if i != 0:
                    for v in range(2):
                        ch = v * NF + j
                        P.op("pool", lambda e, v=v, sl=sl, ch=ch, par=par: e.tensor_tensor(
                            out=cv[sl][:, v, 0:2], in0=cv[sl][:, v, 0:2], in1=fix[1 - par][:, ch, :], op=ALU.add),
                            [cv_b[sl][v], fix_b[1 - par]], [cv_b[sl][v]], n=2)
                for k in (1, 0):
                    for v in range(2):
                        ch = v * NF + j
                        sh = 2 - k
                        P.op("dve", lambda e, v=v, sl=sl, ch=ch, k=k, sh=sh: e.scalar_tensor_tensor(
                            out=cv[sl][:, v, sh:T], in0=pu[sl][v][:, 0:T - sh], scalar=wcol(k, ch), in1=cv[sl][:, v, sh:T],
                            op0=ALU.mult, op1=ALU.add),
                            [pu_b[sl][v], cv_b[sl][v], cols_b], [cv_b[sl][v]])
                for v in range(2):
                    ch = v * NF + j
                    P.op("dve", lambda e, v=v, sl=sl, ch=ch, par=par: e.tensor_copy(out=hs[par][:, ch, :],
                                                                                   in_=pu[sl][v][:, T - 2:T]),
                         [pu_b[sl][v], cv_b[sl][v]], [hs_b[par]], n=2)
                if j >= 1:
                    stageB(j - 1)
                if nxt and j in (12, 17):
                    pend.append(norm_part_a(P, c, N, blk_src(ti + 1, len(pend))))
            stageB(NF - 1)
            if nxt and tiles[ti + 1][1] != 0:
                P.op("dve", lambda e, par=par: e.tensor_tensor(out=fix[par][:, :, 0], in0=hs[par][:, :, 1],
                                                                in1=cols[:, NC2:2 * NC2], op=ALU.mult),
                     [hs_b[par], cols_b], [fix_b[par]], n=NC2)
                P.op("dve", lambda e, par=par: e.tensor_tensor(out=ftmp[:], in0=hs[par][:, :, 0], in1=cols[:, 0:NC2],
                                                                op=ALU.mult), [hs_b[par], cols_b], [ftmp_b], n=NC2)
                P.op("dve", lambda e, par=par: e.tensor_tensor(out=fix[par][:, :, 0], in0=fix[par][:, :, 0], in1=ftmp[:],
                                                                op=ALU.add), [fix_b[par], ftmp_b], [fix_b[par]], n=NC2)
                P.op("dve", lambda e, par=par: e.tensor_tensor(out=fix[par][:, :, 1], in0=hs[par][:, :, 1],
                                                                in1=cols[:, 0:NC2], op=ALU.mult),
                     [hs_b[par], cols_b], [fix_b[par]], n=NC2)
            rsl = []
            for b in range(2):
                rs = rcnt % NXR
                rcnt += 1
                rsl.append(rs)
                P.dma("sp", xres[rs][:], blk_src(ti, b), [], [xres_b[rs]], xres_sem[rs])
            for b in range(NB):
                rs = rsl[b]
                for o in range(2):
                    q = (2 * b + o) % 3
                    for j in range(NF):
                        P.op("pe", lambda e, j=j, b=b, o=o, q=q: e.matmul(
                            pd[q][:], lhsT=gT[:, j, b * 128:(b + 1) * 128], rhs=Wdn[:, j, o * 512:(o + 1) * 512],
                            start=(j == 0), stop=(j == NF - 1)),
                            [gT_b[j], Wdn_b[0 if j < 11 else 1]], [pd_b[q]])
                    P.op("dve", lambda e, rs=rs, o=o, q=q: e.tensor_tensor(
                        out=xres[rs][:, o * 512:(o + 1) * 512], in0=pd[q][:], in1=xres[rs][:, o * 512:(o + 1) * 512],
                        op=ALU.add), [pd_b[q], xres_b[rs]], [xres_b[rs]])
                    if nxt and o == 0:
                        norm_part_b(P, c, N, pend[b], b)
                        if b + 2 < NB:
                            pend.append(norm_part_a(P, c, N, blk_src(ti + 1, b + 2)))
                t0 = i * T + b * 128
                P.dma("sp", dst[s, t0:t0 + 128, :], xres[rs][:], [xres_b[rs]], [], xres_sem[rs])
                if b + 2 < NB:
                    rs2 = rcnt % NXR
                    rcnt += 1
                    rsl.append(rs2)
                    P.dma("sp", xres[rs2][:], blk_src(ti, b + 2), [], [xres_b[rs2]], xres_sem[rs2])
        P.barrier()


def pool_phase(P, c, li, src, dst):
    j = li // 3
    with ExitStack() as st:
        N = alloc_norm(P, st, c, c.w["norm_mix"][li:li + 1, :], want_hT=False, n_xnb=6)
        Wp = P.sbuf(st, "wp", [128, 4, 2, 256], BF16)
        Wp_b = Buf("wp")
        P.dma("pool", Wp[:], c.w["pool_w"][j].rearrange("g (kk p) d -> p g kk d", p=128), [], [Wp_b], P.new_sem())
        sc = P.sbuf(st, "sc_bc", [128, D], F32)
        bs = P.sbuf(st, "bs_bc", [128, D], F32)
        sc_b, bs_b = Buf("sc"), Buf("bs")
        P.dma("sp", sc[:], c.w["pool_scale"][j:j + 1, :].partition_broadcast(128), [], [sc_b], P.new_sem())
        P.dma("sp", bs[:], c.w["pool_b"][j:j + 1].rearrange("o g c -> o (g c)").partition_broadcast(128), [], [bs_b],
              P.new_sem())
        P.op("pool", lambda e: e.tensor_tensor(out=bs[:], in0=bs[:], in1=sc[:], op=ALU.mult), [sc_b, bs_b], [bs_b], n=D)
        Bcur = P.sbuf(st, "Bcur", [128, 4, 128], BF16)
        Bprev = P.sbuf(st, "Bprev", [128, 4, 128], BF16)
        Bfirst = P.sbuf(st, "Bfirst", [128, 4, 128], BF16)
        band_b = Buf("band")
        tmpf = P.sbuf(st, "band_tmp", [128, 128], F32)
        invr = P.sbuf(st, "band_inv", [128, 128], F32)
        tmp_b2 = Buf("band_tmp")
        for g in range(4):
            w = 2 << g
            P.op("pool", lambda e, w=w: e.memset(tmpf[:], 1.0 / w), [], [tmp_b2], n=128)
            P.op("pool", lambda e: e.affine_select(out=tmpf[:], in_=tmpf[:], pattern=[[1, 128]], compare_op=ALU.is_ge,
                                                   fill=0.0, base=0, channel_multiplier=-1), [tmp_b2], [tmp_b2], n=128)
            P.op("pool", lambda e, w=w: e.affine_select(out=tmpf[:], in_=tmpf[:], pattern=[[-1, 128]], compare_op=ALU.is_ge,
                                                        fill=0.0, base=w - 1, channel_multiplier=1), [tmp_b2], [tmp_b2], n=128)
            P.op("pool", lambda e, g=g: e.tensor_tensor(out=Bcur[:, g, :], in0=tmpf[:], in1=c.idf[:], op=ALU.subtract),
                 [tmp_b2, c.const_b], [band_b], n=128)
            P.op("pool", lambda e, w=w: e.memset(tmpf[:], 1.0 / w), [band_b], [tmp_b2], n=128)
            P.op("pool", lambda e, w=w: e.affine_select(out=tmpf[:], in_=tmpf[:], pattern=[[-1, 128]], compare_op=ALU.is_ge,
                                                        fill=0.0, base=-(129 - w), channel_multiplier=1),
                 [tmp_b2], [tmp_b2], n=128)
            P.op("pool", lambda e, g=g: e.tensor_copy(out=Bprev[:, g, :], in_=tmpf[:]), [tmp_b2], [band_b], n=128)
            P.op("pool", lambda e: e.memset(tmpf[:], 1.0), [band_b], [tmp_b2], n=128)
            P.op("pool", lambda e: e.affine_select(out=tmpf[:], in_=tmpf[:], pattern=[[1, 128]], compare_op=ALU.is_ge,
                                                   fill=0.0, base=0, channel_multiplier=-1), [tmp_b2], [tmp_b2], n=128)
            P.op("pool", lambda e, w=w: e.affine_select(out=tmpf[:], in_=tmpf[:], pattern=[[-1, 128]], compare_op=ALU.is_ge,
                                                        fill=0.0, base=w - 1, channel_multiplier=1), [tmp_b2], [tmp_b2], n=128)
            P.op("pool", lambda e, w=w: e.memset(invr[:], 1.0 / w), [tmp_b2], [tmp_b2], n=128)
            for t in range(w - 1):
                P.op("pool", lambda e, t=t: e.memset(invr[:, t:t + 1], 1.0 / (t + 1)), [tmp_b2], [tmp_b2], n=1)
            P.op("pool", lambda e: e.tensor_tensor(out=tmpf[:], in0=tmpf[:], in1=invr[:], op=ALU.mult), [tmp_b2], [tmp_b2],
                 n=128)
            P.op("pool", lambda e, g=g: e.tensor_tensor(out=Bfirst[:, g, :], in0=tmpf[:], in1=c.idf[:], op=ALU.subtract),
                 [tmp_b2, c.const_b], [band_b], n=128)
        pp = [P.psum(st, "pp%d" % i, [128, D], F32) for i in range(2)]
        pp_b = [Buf("pp") for i in range(2)]
        po = [P.psum(st, "po%d" % i, [128, D], F32) for i in range(2)]
        po_b = [Buf("po") for i in range(2)]
        pT = [P.sbuf(st, "pT%d" % i, [128, NCH, 128], BF16) for i in range(2)]
        pT_b = [Buf("pT%d" % i) for i in range(2)]
        NXR = 4
        xres = [P.sbuf(st, "xres%d" % i, [128, D], F32) for i in range(NXR)]
        xres_b = [Buf("xres%d" % i) for i in range(NXR)]
        xres_sem = [P.new_sem() for _ in range(NXR)]
        tmp = [P.sbuf(st, "ptmp%d" % i, [128, D], F32) for i in range(2)]
        tmp_b = [Buf("ptmp") for i in range(2)]
        blocks = [(s, ib) for s in range(c.n_seq) for ib in range(c.S // 128)]

        def part_a(n):
            s, ib = blocks[n]
            return norm_part_a(P, c, N, src[s, ib * 128:(ib + 1) * 128, :])

        LA = 4
        pend_store = []

        def flush_store():
            s_, ib_, rs_ = pend_store.pop(0)
            P.dma("act", dst[s_, ib_ * 128:(ib_ + 1) * 128, :], xres[rs_][:], [xres_b[rs_]], [], xres_sem[rs_])

        slots = {}
        for n in range(min(LA, len(blocks))):
            slots[n] = part_a(n)
        for n, (s, ib) in enumerate(blocks):
            if n + LA < len(blocks):
                slots[n + LA] = part_a(n + LA)
            sl = n % 2
            rs = n % NXR
            xc = N.xnb[slots[n]]
            xc_b = N.xnb_b[slots[n]]
            P.dma("sp", xres[rs][:], src[s, ib * 128:(ib + 1) * 128, :], [], [xres_b[rs]], xres_sem[rs])
            P.op("pool", lambda e, rs=rs: e.tensor_tensor(out=xres[rs][:], in0=xres[rs][:], in1=bs[:], op=ALU.add),
                 [xres_b[rs], bs_b], [xres_b[rs]], n=D)
            for k in range(NCH):
                g = k // 2
                if ib == 0:
                    P.op("pe", lambda e, k=k, g=g, sl=sl, xc=xc: e.matmul(
                        pp[sl][:, k * 128:(k + 1) * 128], lhsT=xc[:, k * 128:(k + 1) * 128], rhs=Bfirst[:, g, :],
                        start=True, stop=True), [xc_b, band_b], [pp_b[sl]], n=128)
                else:
                    xp = N.xnb[slots[n - 1]]
                    xp_b = N.xnb_b[slots[n - 1]]
                    P.op("pe", lambda e, k=k, g=g, sl=sl, xc=xc: e.matmul(
                        pp[sl][:, k * 128:(k + 1) * 128], lhsT=xc[:, k * 128:(k + 1) * 128], rhs=Bcur[:, g, :],
                        start=True, stop=False), [xc_b, band_b], [pp_b[sl]], n=128)
                    P.op("pe", lambda e, k=k, g=g, sl=sl, xp=xp: e.matmul(
                        pp[sl][:, k * 128:(k + 1) * 128], lhsT=xp[:, k * 128:(k + 1) * 128], rhs=Bprev[:, g, :],
                        start=False, stop=True), [xp_b, band_b], [pp_b[sl]], n=128)
            P.op("act", lambda e, sl=sl: e.copy(out=pT[sl][:], in_=pp[sl][:].rearrange("p (k t) -> p k t", k=NCH)),
                 [pp_b[sl]], [pT_b[sl]], n=D)
            for g in range(4):
                for kk in range(2):
                    P.op("pe", lambda e, g=g, kk=kk, sl=sl: e.matmul(
                        po[sl][:, g * 256:(g + 1) * 256], lhsT=pT[sl][:, 2 * g + kk, :], rhs=Wp[:, g, kk, :],
                        start=(kk == 0), stop=(kk == 1)), [pT_b[sl], Wp_b], [po_b[sl]])
            P.op("dve", lambda e, sl=sl: e.tensor_tensor(out=tmp[sl][:], in0=po[sl][:], in1=sc[:], op=ALU.mult),
                 [po_b[sl], sc_b], [tmp_b[sl]], n=D)
            P.op("pool", lambda e, rs=rs, sl=sl: e.tensor_tensor(out=xres[rs][:], in0=xres[rs][:], in1=tmp[sl][:], op=ALU.add),
                 [xres_b[rs], tmp_b[sl]], [xres_b[rs]], n=D)
            pend_store.append((s, ib, rs))
            if len(pend_store) > 1:
                flush_store()
        while pend_store:
            flush_store()
        P.barrier()


def conv_phase(P, c, li, src, dst):
    j = li // 3
    HS = CONVW - 1
    with ExitStack() as st:
        N = alloc_norm(P, st, c, c.w["norm_mix"][li:li + 1, :], n_xnb=4)
        Win = [P.sbuf(st, "win%d" % k, [128, 2 * D], BF16) for k in range(NCH)]
        Win_b = [Buf("win%d" % k) for k in range(NCH)]
        for k in range(NCH):
            P.dma("pool", Win[k][:], c.w["conv_w_in"][j, k * 128:(k + 1) * 128, :], [], [Win_b[k]], P.new_sem(),
                  max_dma_last_dim=4096)
        Wout = P.sbuf(st, "wout", [128, NCH, D], BF16)
        Wout_b = Buf("wout")
        P.dma("pool", Wout[:], c.w["conv_w_out"][j].rearrange("(k p) d -> p k d", p=128), [], [Wout_b], P.new_sem(),
              max_dma_last_dim=4096)
        bo = P.sbuf(st, "bo_bc", [128, D], F32)
        bo_b = Buf("bo")
        P.dma("sp", bo[:], c.w["conv_b_out"][j:j + 1, :].partition_broadcast(128), [], [bo_b], P.new_sem())
        B = [P.psum(st, "B%d" % i, [128, T], F32) for i in range(7)]
        B_b = [Buf("B%d" % i) for i in range(7)]
        cols, cols_b = load_cols(P, st, "ccols",
                                 [c.w["conv_b_in"][j].rearrange("(c p) -> c p", p=128),
                                  c.w["conv_dw"][j].rearrange("k (c p) -> (k c) p", p=128),
                                  c.w["conv_dw_b"][j].rearrange("(c p) -> c p", p=128),
                                  c.w["conv_ln_g"][j].rearrange("(c p) -> c p", p=128),
                                  c.w["conv_ln_b"][j].rearrange("(c p) -> c p", p=128)],
                                 B[0], B_b[0], c)
        col = lambda i: cols[:, i:i + 1]
        bin_col = lambda ch: col(ch)
        dw_col = lambda k, ch: col(16 + k * NCH + ch)
        dwb_col = lambda ch: col(16 + CONVW * NCH + ch)
        lng_col = lambda ch: col(16 + CONVW * NCH + 8 + ch)
        lnb_col = lambda ch: col(16 + CONVW * NCH + 16 + ch)
        Dg = P.sbuf(st, "Dg", [128, NCH, CONVW, 128], BF16)
        Dg_b = [[Buf("Dg") for k in range(CONVW)] for ch in range(NCH)]

        def build_dg():
            n = 0
            for ch in range(NCH):
                for k in range(CONVW):
                    eng = "dve" if n % 2 == 0 else "pool"
                    n += 1
                    if eng == "dve":
                        P.op(eng, lambda e, ch=ch, k=k: e.tensor_scalar(out=Dg[:, ch, k, :], in0=c.idb[:], scalar1=dw_col(k, ch),
                                                                        scalar2=None, op0=ALU.mult),
                             [c.const_b, cols_b], [Dg_b[ch][k]], n=128)
                    else:
                        P.op(eng, lambda e, ch=ch, k=k: e.tensor_scalar(out=Dg[:, ch, k, :], in0=c.idb[:], scalar1=dw_col(k, ch),
                                                                        scalar2=1.0, op0=ALU.mult, op1=ALU.mult),
                             [c.const_b, cols_b], [Dg_b[ch][k]], n=128)
        onesM = P.sbuf(st, "onesM", [128, 128], BF16)
        onesM_b = Buf("onesM")
        P.op("pool", lambda e: e.memset(onesM[:], 1.0 / D), [], [onesM_b], n=128)
        Vb = P.sbuf(st, "Vb", [128, NCH, HS + T], BF16)
        Vh_b = Buf("Vh")
        Vm_b = [Buf("Vm%d" % k) for k in range(NCH)]
        cub = P.sbuf(st, "cub", [128, NCH, T], BF16)
        cub_b = [Buf("cub%d" % k) for k in range(NCH)]
        sq = P.sbuf(st, "sq", [128, NCH, T], BF16)
        sq_b = [Buf("sq%d" % k) for k in range(NCH)]
        sT = P.sbuf(st, "sT", [128, NCH, T], BF16)
        sT_b = [Buf("sT%d" % k) for k in range(NCH)]
        sgt = [P.sbuf(st, "sgt%d" % i, [128, T], F32) for i in range(2)]
        sgt_b = [Buf("sgt") for i in range(2)]
        mean_sb = P.sbuf(st, "mean_sb", [128, T], F32)
        var = P.sbuf(st, "var", [128, T], F32)
        rstd = P.sbuf(st, "rstd", [128, T], F32)
        mean_b, var_b, rstd_b = Buf("mean"), Buf("var"), Buf("rstd")
        t1 = [P.sbuf(st, "t1%d" % i, [128, T], F32) for i in range(2)]
        t1_b = [Buf("t1") for i in range(2)]
        sg2 = [P.sbuf(st, "sg2%d" % i, [128, T], F32) for i in range(2)]
        sg2_b = [Buf("sg2") for i in range(2)]
        NXR = 4
        xres = [P.sbuf(st, "xres%d" % i, [128, 512], F32) for i in range(NXR)]
        xres_b = [Buf("xres%d" % i) for i in range(NXR)]
        xres_sem = [P.new_sem() for _ in range(NXR)]
        tiles = tile_list(c)
        rc = [0]

        def norm_a(ti):
            s, i = tiles[ti]
            return [norm_part_a(P, c, N, src[s, i * T + b * 128:i * T + (b + 1) * 128, :]) for b in range(NB)]

        def norm_b(slots):
            for b in range(NB):
                norm_part_b(P, c, N, slots[b], b)

        def in_head(ti):
            s, i = tiles[ti]
            if i == 0:
                P.op("pool", lambda e: e.memset(Vb[:, :, 0:HS], 0.0), [], [Vh_b], n=HS * NCH)

        def in_pair(ti, ch):
            if True:
                sl = ch % 2
                pa, pg = B[2 * sl], B[2 * sl + 1]
                for v, pp in ((0, pa), (1, pg)):
                    cc = v * NCH + ch
                    for k in range(NCH):
                        P.op("pe", lambda e, pp=pp, k=k, cc=cc: e.matmul(
                            pp[:], lhsT=Win[k][:, cc * 128:(cc + 1) * 128], rhs=N.hT[:, k, :],
                            start=(k == 0), stop=(k == NCH - 1)), [Win_b[k]] + N.hT_b, [B_b[2 * sl + v]])
                P.op("act", lambda e, sl=sl, pg=pg, ch=ch: e.activation(out=sgt[sl][:], in_=pg[:], func=AF.Sigmoid,
                                                                        bias=bin_col(NCH + ch), scale=1.0),
                     [B_b[2 * sl + 1], cols_b], [sgt_b[sl]])
                P.op("dve", lambda e, sl=sl, pa=pa, ch=ch: e.scalar_tensor_tensor(
                    out=Vb[:, ch, HS:HS + T], in0=pa[:], scalar=bin_col(ch), in1=sgt[sl][:], op0=ALU.add, op1=ALU.mult),
                    [B_b[2 * sl], sgt_b[sl], cols_b], [Vm_b[ch]])

        def stage_conv(ti):
            s, i = tiles[ti]
            for ch in range(NCH):
                pc = B[4 + ch % 2]
                pcb = B_b[4 + ch % 2]
                for k in range(CONVW):
                    P.op("pe", lambda e, pc=pc, ch=ch, k=k: e.matmul(
                        pc[:], lhsT=Dg[:, ch, k, :], rhs=Vb[:, ch, k:k + T], start=(k == 0), stop=(k == CONVW - 1)),
                        [Dg_b[ch][k], Vh_b, Vm_b[ch]], [pcb])
                P.op("act", lambda e, pc=pc, ch=ch: e.activation(out=cub[:, ch, :], in_=pc[:], func=AF.Identity,
                                                                 bias=dwb_col(ch), scale=1.0),
                     [pcb, cols_b], [cub_b[ch]])
                P.op("act", lambda e, pc=pc, ch=ch: e.activation(out=sq[:, ch, :], in_=pc[:], func=AF.Square,
                                                                 bias=dwb_col(ch), scale=1.0),
                     [pcb, cols_b], [sq_b[ch]])
            if ti + 1 < len(tiles) and tiles[ti + 1][1] != 0:
                P.op("pool", lambda e: e.tensor_copy(out=Vb[:, :, 0:HS], in_=Vb[:, :, T:T + HS]), Vm_b, [Vh_b], n=HS * NCH)

        def stage_stats(ti):
            pm, pq = B[6], B[0]
            for ch in range(NCH):
                P.op("pe", lambda e, ch=ch: e.matmul(pm[:], lhsT=onesM[:], rhs=cub[:, ch, :], start=(ch == 0),
                                                     stop=(ch == NCH - 1)), [onesM_b, cub_b[ch]], [B_b[6]])
            for ch in range(NCH):
                P.op("pe", lambda e, ch=ch: e.matmul(pq[:], lhsT=onesM[:], rhs=sq[:, ch, :], start=(ch == 0),
                                                     stop=(ch == NCH - 1)), [onesM_b, sq_b[ch]], [B_b[0]])
            P.op("act", lambda e: e.copy(out=mean_sb[:], in_=pm[:]), [B_b[6]], [mean_b])
            P.op("act", lambda e: e.activation(out=var[:], in_=pm[:], func=AF.Square), [B_b[6]], [var_b])
            P.op("dve", lambda e: e.tensor_tensor(out=var[:], in0=pq[:], in1=var[:], op=ALU.subtract), [B_b[0], var_b], [var_b])
            P.op("act", lambda e: e.activation(out=var[:], in_=var[:], func=AF.Sqrt, bias=c.eps_col[:, 0:1], scale=1.0),
                 [var_b, c.const_b], [var_b])
            P.op("dve", lambda e: e.reciprocal(out=rstd[:], in_=var[:]), [var_b], [rstd_b])

        def ln_chunk(ti, ch):
            if True:
                sl = ch % 2
                P.op("pool", lambda e, sl=sl, ch=ch: e.tensor_tensor(out=t1[sl][:], in0=cub[:, ch, :], in1=mean_sb[:],
                                                                     op=ALU.subtract), [cub_b[ch], mean_b], [t1_b[sl]])
                P.op("dve", lambda e, sl=sl: e.tensor_tensor(out=t1[sl][:], in0=t1[sl][:], in1=rstd[:], op=ALU.mult),
                     [t1_b[sl], rstd_b], [t1_b[sl]])
                P.op("act", lambda e, sl=sl, ch=ch: e.activation(out=t1[sl][:], in_=t1[sl][:], func=AF.Identity,
                                                                 scale=lng_col(ch), bias=lnb_col(ch)),
                     [t1_b[sl], cols_b], [t1_b[sl]])
                P.op("act", lambda e, sl=sl, ch=ch: e.activation(out=sg2[sl][:], in_=t1[sl][:], func=AF.Sigmoid),
                     [t1_b[sl]], [sg2_b[sl]])
                P.op("pool", lambda e, sl=sl, ch=ch: e.tensor_tensor(out=sT[:, ch, :], in0=t1[sl][:], in1=sg2[sl][:],
                                                                     op=ALU.mult), [t1_b[sl], sg2_b[sl]], [sT_b[ch]])

        def stage_out(ti):
            s, i = tiles[ti]
            grp = [(b, o) for b in range(NB) for o in range(2)]

            def ld(gi):
                b, o = grp[gi]
                t0 = i * T + b * 128
                rs = rc[0] % NXR
                rc[0] += 1
                P.dma("sp", xres[rs][:], src[s, t0:t0 + 128, o * 512:(o + 1) * 512], [], [xres_b[rs]], xres_sem[rs])
                P.op("pool", lambda e, rs=rs, o=o: e.tensor_tensor(out=xres[rs][:], in0=xres[rs][:],
                                                                   in1=bo[:, o * 512:(o + 1) * 512], op=ALU.add),
                     [xres_b[rs], bo_b], [xres_b[rs]], n=512)
                return rs

            rsl = [ld(0), ld(1), ld(2)]
            for gi, (b, o) in enumerate(grp):
                t0 = i * T + b * 128
                rs = rsl[gi]
                q = gi % 4
                for ch in range(NCH):
                    P.op("pe", lambda e, ch=ch, b=b, o=o, q=q: e.matmul(
                        B[q][:], lhsT=sT[:, ch, b * 128:(b + 1) * 128], rhs=Wout[:, ch, o * 512:(o + 1) * 512],
                        start=(ch == 0), stop=(ch == NCH - 1)), [sT_b[ch], Wout_b], [B_b[q]])
                P.op("dve", lambda e, rs=rs, q=q: e.tensor_tensor(out=xres[rs][:], in0=B[q][:], in1=xres[rs][:], op=ALU.add),
                     [B_b[q], xres_b[rs]], [xres_b[rs]])
                P.dma("sp", dst[s, t0:t0 + 128, o * 512:(o + 1) * 512], xres[rs][:], [xres_b[rs]], [], xres_sem[rs])
                if gi + 3 < len(grp):
                    rsl.append(ld(gi + 3))

        norm_b(norm_a(0))
        in_head(0)
        for ch in range(NCH):
            in_pair(0, ch)
        build_dg()
        for ti in range(len(tiles)):
            nxt = ti + 1 < len(tiles)
            if nxt:
                slots = norm_a(ti + 1)
            stage_conv(ti)
            stage_stats(ti)
            if nxt:
                norm_b(slots)
                in_head(ti + 1)
            for ch in range(NCH):
                if nxt:
                    in_pair(ti + 1, ch)
                ln_chunk(ti, ch)
            stage_out(ti)
        P.barrier()


def fox_phase(P, c, li, src, dst):
    nc = P.nc
    j = li // 3
    S = c.S
    NT = S // T
    NKB = S // 128
    tiles = tile_list(c)
    w_in = c.w["fox_w_in"][j]
    with ExitStack() as st:
        N = alloc_norm(P, st, c, c.w["norm_mix"][li:li + 1, :], n_xnb=4)
        hT2 = P.sbuf(st, "hT2", [128, NCH, T], BF16)
        hTs = [N.hT, hT2]
        hTs_b = [N.hT_b, [Buf("hT2_%d" % i) for i in range(NB)]]
        WC = 3 * D + NH
        Wi = [P.sbuf(st, "wi%d" % k, [128, WC], BF16) for k in range(NCH)]
        Wi_b = [Buf("wi%d" % k) for k in range(NCH)]
        for k in range(NCH):
            P.dma("pool", Wi[k][:], w_in[k * 128:(k + 1) * 128, :], [], [Wi_b[k]], P.new_sem(), max_dma_last_dim=4096)
        gcol = P.sbuf(st, "gcol", [128, 2], F32)
        gcol_b = Buf("gcol")
        gsem = P.new_sem()
        for half in range(2):
            P.dma("sp", gcol[half * 64:(half + 1) * 64, 0:1], c.w["fox_q_gain"][j].rearrange("(h o) -> h o", o=1), [], [gcol_b], gsem)
            P.dma("sp", gcol[half * 64:(half + 1) * 64, 1:2], c.w["fox_k_gain"][j].rearrange("(h o) -> h o", o=1), [], [gcol_b], gsem)
        P.op("pool", lambda e: e.tensor_scalar(out=gcol[:, 0:1], in0=gcol[:, 0:1], scalar1=0.125, scalar2=1.0,
                                               op0=ALU.mult, op1=ALU.mult), [gcol_b], [gcol_b], n=1)
        bfc = P.sbuf(st, "bfc", [NH, 1], F32)
        bfc_b = Buf("bfc")
        P.dma("sp", bfc[:, 0:1], c.w["fox_b_f"][j].rearrange("(h o) -> h o", o=1), [], [bfc_b], P.new_sem())
        P.op("pool", lambda e: e.tensor_scalar(out=bfc[:], in0=bfc[:], scalar1=-1.0, scalar2=1.0, op0=ALU.mult, op1=ALU.mult),
             [bfc_b], [bfc_b], n=1)
        blk = P.sbuf(st, "blk", [128, 128], BF16)
        blk_b = Buf("blk")
        P.op("pool", lambda e: e.memset(blk[:], 0.0), [], [blk_b], n=128)
        P.op("pool", lambda e: e.memset(blk[0:64, 0:64], 1.0 / HD), [blk_b], [blk_b], n=64)
        P.op("pool", lambda e: e.memset(blk[64:128, 64:128], 1.0 / HD), [blk_b], [blk_b], n=64)
        ones16 = P.sbuf(st, "ones16", [NH, T], F32)
        ones16_b = Buf("ones16")
        P.op("pool", lambda e: e.memset(ones16[:], 1.0), [], [ones16_b], n=T)
        pq = [P.psum(st, "pq%d" % i, [128, T], F32) for i in range(3)]
        pq_b = [Buf("pq") for i in range(3)]
        pm = [P.psum(st, "pm%d" % i, [128, T], F32) for i in range(2)]
        pm_b = [Buf("pm") for i in range(2)]
        pv = [P.psum(st, "pv%d" % i, [128, T], F32) for i in range(2)]
        pv_b = [Buf("pv") for i in range(2)]
        sqh = [P.sbuf(st, "sqh%d" % i, [128, T], BF16) for i in range(2)]
        sqh_b = [Buf("sqh") for i in range(2)]
        ln1 = [P.sbuf(st, "ln1%d" % i, [128, T], F32) for i in range(2)]
        ln1_b = [Buf("ln1") for i in range(2)]
        rs = [P.sbuf(st, "rs%d" % i, [128, T], F32) for i in range(2)]
        rs_b = [Buf("rs") for i in range(2)]
        qn = [P.sbuf(st, "qn%d" % i, [128, T], BF16) for i in range(4)]
        qn_b = [Buf("qn") for i in range(4)]
        qn_sem = [P.new_sem() for _ in range(4)]
        vb = [P.sbuf(st, "vb%d" % i, [128, D], BF16) for i in range(2)]
        vb_b = [Buf("vb") for i in range(2)]
        vb_sem = [P.new_sem() for _ in range(2)]
        e1 = P.sbuf(st, "e1", [NH, T], F32)
        e1_b = Buf("e1")
        cc_t = [P.sbuf(st, "cc%d" % i, [NH, T], F32) for i in range(2)]
        cc_b = [Buf("cc") for i in range(2)]
        r1 = P.sbuf(st, "r1", [NH, T], F32)
        r1_b = Buf("r1")
        cq = [P.sbuf(st, "cq%d" % i, [NH, 3, T], BF16) for i in range(2)]
        ck = [P.sbuf(st, "ck%d" % i, [NH, 3, T], BF16) for i in range(2)]
        cq_b = [Buf("cq") for i in range(2)]
        cq_sem = [P.new_sem() for _ in range(2)]
        cnt = {"qn": 0, "vb": 0}
        def f1_norm_a(ti):
            s, i = tiles[ti]
            return [norm_part_a(P, c, N, src[s, i * T + b * 128:i * T + (b + 1) * 128, :]) for b in range(NB)]

        def f1_norm_b(ti, slots, b):
            hs_ = ti % 2
            norm_part_b(P, c, N, slots[b], b, out_ap=hTs[hs_][:, :, b * 128:(b + 1) * 128], out_bufs=[hTs_b[hs_][b]])

        sl0 = f1_norm_a(0)
        for b in range(NB):
            f1_norm_b(0, sl0, b)
        for ti, (s, i) in enumerate(tiles):
            t0 = i * T
            hT = hTs[ti % 2]
            hT_b = hTs_b[ti % 2]
            nxt = ti + 1 < len(tiles)
            if nxt:
                nslots = f1_norm_a(ti + 1)
            def proj(cc):
                sl = cc % 3
                for k in range(NCH):
                    P.op("pe", lambda e, sl=sl, k=k, cc=cc, hT=hT: e.matmul(
                        pq[sl][:], lhsT=Wi[k][:, cc * 128:(cc + 1) * 128], rhs=hT[:, k, :],
                        start=(k == 0), stop=(k == NCH - 1)), [Wi_b[k]] + hT_b, [pq_b[sl]])
                s2 = cc % 2
                P.op("act", lambda e, sl=sl, s2=s2: e.activation(out=sqh[s2][:], in_=pq[sl][:], func=AF.Square),
                     [pq_b[sl]], [sqh_b[s2]])

            def finish(cc):
                sl = cc % 3
                s2 = cc % 2
                P.op("pe", lambda e, s2=s2: e.matmul(pm[s2][:], lhsT=blk[:], rhs=sqh[s2][:], start=True, stop=True),
                     [blk_b, sqh_b[s2]], [pm_b[s2]])
                P.op("act", lambda e, s2=s2: e.activation(out=ln1[s2][:], in_=pm[s2][:], func=AF.Ln, bias=c.eps_col[:, 0:1],
                                                          scale=1.0), [pm_b[s2], c.const_b], [ln1_b[s2]])
                P.op("act", lambda e, s2=s2: e.activation(out=rs[s2][:], in_=ln1[s2][:], func=AF.Exp, scale=-0.5),
                     [ln1_b[s2]], [rs_b[s2]])
                qs = cnt["qn"] % 4
                cnt["qn"] += 1
                gi = 0 if cc < NCH else 1
                P.op("dve", lambda e, sl=sl, s2=s2, qs=qs, gi=gi: e.scalar_tensor_tensor(
                    out=qn[qs][:], in0=pq[sl][:], scalar=gcol[:, gi:gi + 1], in1=rs[s2][:], op0=ALU.mult, op1=ALU.mult),
                    [pq_b[sl], rs_b[s2], gcol_b], [qn_b[qs]])
                dd = c.qT_d if cc < NCH else c.kT_d
                r0 = (cc % NCH) * 128
                P.dma("sp", dd[s, r0:r0 + 128, t0:t0 + T], qn[qs][:], [qn_b[qs]], [], qn_sem[qs])

            proj(0)
            for cc in range(2 * NCH):
                if cc + 1 < 2 * NCH:
                    proj(cc + 1)
                finish(cc)
                if nxt and cc % 4 == 3:
                    f1_norm_b(ti + 1, nslots, cc // 4)
            P.op_dummy = None
            for k in range(NCH):
                P.op("pe", lambda e, k=k, hT=hT: e.matmul(pv[0][0:NH, :], lhsT=Wi[k][:, 3 * D:3 * D + NH], rhs=hT[:, k, :],
                                                   start=(k == 0), stop=(k == NCH - 1)), [Wi_b[k]] + hT_b, [pv_b[0]])
            P.op("act", lambda e: e.activation(out=e1[:], in_=pv[0][0:NH, :], func=AF.Exp, scale=-1.0, bias=bfc[:, 0:1]),
                 [pv_b[0], bfc_b], [e1_b])
            P.op("act", lambda e: e.activation(out=e1[:], in_=e1[:], func=AF.Ln, bias=1.0, scale=1.0), [e1_b], [e1_b])
            cs = ti % 2
            if i == 0:
                P.op("dve", lambda e, cs=cs: e.tensor_tensor_scan(out=cc_t[cs][:], data0=ones16[:], data1=e1[:], initial=0.0,
                                                                   op0=ALU.mult, op1=ALU.subtract),
                     [ones16_b, e1_b], [cc_b[cs]])
            else:
                P.op("dve", lambda e, cs=cs: e.tensor_tensor_scan(out=cc_t[cs][:], data0=ones16[:], data1=e1[:],
                                                                   initial=cc_t[1 - cs][:, T - 1:T], op0=ALU.mult,
                                                                   op1=ALU.subtract),
                     [ones16_b, e1_b, cc_b[1 - cs]], [cc_b[cs]])
            P.op("pool", lambda e, cs=cs: e.tensor_copy(out=cq[cs][:, 0, :], in_=cc_t[cs][:]), [cc_b[cs]], [cq_b[cs]])
            P.op("pool", lambda e, cs=cs: e.tensor_tensor(out=r1[:], in0=cc_t[cs][:], in1=cq[cs][:, 0, :], op=ALU.subtract),
                 [cc_b[cs], cq_b[cs]], [r1_b])
            P.op("pool", lambda e, cs=cs: e.tensor_copy(out=cq[cs][:, 1, :], in_=r1[:]), [r1_b], [cq_b[cs]])
            P.op("pool", lambda e, cs=cs: e.tensor_tensor(out=r1[:], in0=r1[:], in1=cq[cs][:, 1, :], op=ALU.subtract),
                 [r1_b, cq_b[cs]], [r1_b])
            P.op("pool", lambda e, cs=cs: e.tensor_copy(out=cq[cs][:, 2, :], in_=r1[:]), [r1_b], [cq_b[cs]])
            P.op("pool", lambda e, cs=cs: e.tensor_scalar(out=ck[cs][:], in0=cq[cs][:], scalar1=-1.0, scalar2=1.0,
                                                          op0=ALU.mult, op1=ALU.mult), [cq_b[cs]], [cq_b[cs]], n=3 * T)
            P.dma("sp", c.cq_d[s, :, :, t0:t0 + T], cq[cs][:], [cq_b[cs]], [], cq_sem[cs])
            P.dma("sp", c.ck_d[s, :, :, t0:t0 + T], ck[cs][:], [cq_b[cs]], [], cq_sem[cs])
            for b in range(NB):
                vs = cnt["vb"] % 2
                cnt["vb"] += 1
                for o in range(2):
                    for k in range(NCH):
                        P.op("pe", lambda e, k=k, b=b, o=o, hT=hT: e.matmul(
                            pv[1][:], lhsT=hT[:, k, b * 128:(b + 1) * 128],
                            rhs=Wi[k][:, 2 * D + o * 512:2 * D + (o + 1) * 512], start=(k == 0), stop=(k == NCH - 1)),
                            [Wi_b[k], hT_b[b]], [pv_b[1]])
                    P.op("act", lambda e, vs=vs, o=o: e.copy(out=vb[vs][:, o * 512:(o + 1) * 512], in_=pv[1][:]),
                         [pv_b[1]], [vb_b[vs]])
                P.dma("sp", c.v_d[s, t0 + b * 128:t0 + (b + 1) * 128, :], vb[vs][:], [vb_b[vs]], [], vb_sem[vs])
        P.barrier()
    with ExitStack() as st:
        KR = HD + 6
        QA = [P.sbuf(st, "QA%d" % i, [KR, S], BF16) for i in range(2)]
        KA = [P.sbuf(st, "KA%d" % i, [KR, S], BF16) for i in range(2)]
        VA = [P.sbuf(st, "VA%d" % i, [128, NKB, 128], BF16) for i in range(2)]
        hd_b = [Buf("head%d" % i) for i in range(2)]
        hd_sem = [P.new_sem() for _ in range(2)]
        for i in range(2):
            P.op("pool", lambda e, i=i: e.memset(QA[i][HD:KR, :], 1.0), [], [hd_b[i]], n=S)
            P.op("pool", lambda e, i=i: e.memset(KA[i][HD:KR, :], 1.0), [], [hd_b[i]], n=S)
            P.op("pool", lambda e, i=i: e.memset(VA[i][:, :, HD:128], 1.0), [], [hd_b[i]], n=NKB * 64)
        NPS = 6
        ps = [P.psum(st, "ps%d" % i, [128, T], F32) for i in range(NPS)]
        ps_b = [Buf("ps") for i in range(NPS)]
        po = [P.psum(st, "po%d" % i, [128, T], F32) for i in range(2)]
        po_b = [Buf("po") for i in range(2)]
        pt = [P.sbuf(st, "pt%d" % i, [128, T], BF16) for i in range(NPS)]
        pt_b = [Buf("pt") for i in range(NPS)]
        rec = [P.sbuf(st, "rec%d" % i, [HD, T], F32) for i in range(2)]
        rec_b = [Buf("rec") for i in range(2)]
        ob = [P.sbuf(st, "ob%d" % i, [HD, T], BF16) for i in range(2)]
        ob_b = [Buf("ob") for i in range(2)]
        ob_sem = [P.new_sem() for _ in range(2)]
        heads = [(s, h) for s in range(c.n_seq) for h in range(NH)]

        def load_head(n):
            s, h = heads[n]
            sl = n % 2
            P.dma("sp", QA[sl][0:HD, :], c.qT_d[s, h * HD:(h + 1) * HD, :], [], [hd_b[sl]], hd_sem[sl])
            P.dma("sp", QA[sl][HD:HD + 3, :], c.cq_d[s, h, :, :], [], [hd_b[sl]], hd_sem[sl])
            P.dma("sp", KA[sl][0:HD, :], c.kT_d[s, h * HD:(h + 1) * HD, :], [], [hd_b[sl]], hd_sem[sl])
            P.dma("sp", KA[sl][HD + 3:HD + 6, :], c.ck_d[s, h, :, :], [], [hd_b[sl]], hd_sem[sl])
            vsrc = c.v_d[s].rearrange("(kb p) f -> p kb f", p=128)
            nsp = max(1, NKB // 8)
            for q in range(nsp):
                k0, k1 = q * NKB // nsp, (q + 1) * NKB // nsp
                P.dma("sp", VA[sl][:, k0:k1, 0:HD], vsrc[:, k0:k1, h * HD:(h + 1) * HD], [], [hd_b[sl]], hd_sem[sl])

        load_head(0)
        LA = 5
        ocnt = 0
        for n, (s, h) in enumerate(heads):
            sl = n % 2
            if n + 1 < len(heads):
                load_head(n + 1)
            items = []
            for qt in range(NT):
                nkb = 4 * qt + 4
                for kb in range(nkb):
                    items.append((qt, kb, kb == 0, kb == nkb - 1))

            def emit_s(ii, sl=sl):
                qt, kb, first, last = items[ii]
                r = kb - 4 * qt
                c0 = max(r, 0) * 128
                p = ii % NPS
                P.op("pe", lambda e, p=p, kb=kb, qt=qt, c0=c0, sl=sl: e.matmul(
                    ps[p][:, c0:T], lhsT=KA[sl][:, kb * 128:(kb + 1) * 128], rhs=QA[sl][:, qt * T + c0:(qt + 1) * T],
                    start=True, stop=True), [hd_b[sl]], [ps_b[p]])

            for ii in range(min(LA, len(items))):
                emit_s(ii)
            for ii, (qt, kb, first, last) in enumerate(items):
                r = kb - 4 * qt
                c0 = max(r, 0) * 128
                p = ii % NPS
                oq = (n * NT + qt) % 2
                P.op("act", lambda e, p=p, c0=c0: e.activation(out=pt[p][:, c0:T], in_=ps[p][:, c0:T], func=AF.Exp),
                     [ps_b[p]], [pt_b[p]], n=T - c0)
                if r >= 0:
                    P.op("pool", lambda e, p=p, c0=c0: e.affine_select(
                        out=pt[p][:, c0:c0 + 128], in_=pt[p][:, c0:c0 + 128], pattern=[[1, 128]], compare_op=ALU.is_ge,
                        fill=0.0, base=0, channel_multiplier=-1), [pt_b[p]], [pt_b[p]], n=128)
                P.op("pe", lambda e, p=p, c0=c0, kb=kb, oq=oq, first=first, last=last, sl=sl: e.matmul(
                    po[oq][:, c0:T], lhsT=VA[sl][:, kb, :], rhs=pt[p][:, c0:T], start=first, stop=last),
                    [hd_b[sl], pt_b[p]], [po_b[oq]])
                if ii + LA < len(items):
                    emit_s(ii + LA)
                if last:
                    os_ = ocnt % 2
                    ocnt += 1
                    P.op("dve", lambda e, oq=oq, os_=os_: e.reciprocal(out=rec[os_][:], in_=po[oq][HD:128, :]),
                         [po_b[oq]], [rec_b[os_]])
                    P.op("dve", lambda e, oq=oq, os_=os_: e.tensor_tensor(out=ob[os_][:], in0=po[oq][0:HD, :], in1=rec[os_][:],
                                                                          op=ALU.mult), [po_b[oq], rec_b[os_]], [ob_b[os_]])
                    P.dma("sp", c.o_d[s, h * HD:(h + 1) * HD, qt * T:(qt + 1) * T], ob[os_][:], [ob_b[os_]], [], ob_sem[os_])
        P.barrier()
    with ExitStack() as st:
        Wo = P.sbuf(st, "wo", [128, NCH, D], BF16)
        Wo_b = Buf("wo")
        P.dma("pool", Wo[:], c.w["fox_w_o"][j].rearrange("(k p) d -> p k d", p=128), [], [Wo_b], P.new_sem(),
              max_dma_last_dim=4096)
        ot = [P.sbuf(st, "ot%d" % i, [128, NCH, T], BF16) for i in range(2)]
        ot_b = [Buf("ot") for i in range(2)]
        ot_sem = [P.new_sem() for _ in range(2)]
        pd = [P.psum(st, "pd%d" % i, [128, T], F32) for i in range(4)]
        pd_b = [Buf("pd") for i in range(4)]
        xres = [P.sbuf(st, "xres%d" % i, [128, D], F32) for i in range(3)]
        xres_b = [Buf("xres%d" % i) for i in range(3)]
        xres_sem = [P.new_sem() for _ in range(3)]
        rcnt = 0

        def load_ot(ti):
            s, i = tiles[ti]
            P.dma("sp", ot[ti % 2][:], c.o_d[s].rearrange("(k p) t -> p k t", p=128)[:, :, i * T:(i + 1) * T], [],
                  [ot_b[ti % 2]], ot_sem[ti % 2])

        load_ot(0)
        for ti, (s, i) in enumerate(tiles):
            if ti + 1 < len(tiles):
                load_ot(ti + 1)
            osl = ti % 2
            for b in range(NB):
                t0 = i * T + b * 128
                rs_ = rcnt % 3
                rcnt += 1
                P.dma("sp", xres[rs_][:], src[s, t0:t0 + 128, :], [], [xres_b[rs_]], xres_sem[rs_])
                for o in range(2):
                    q = (2 * b + o) % 4
                    for k in range(NCH):
                        P.op("pe", lambda e, k=k, b=b, o=o, q=q, osl=osl: e.matmul(
                            pd[q][:], lhsT=ot[osl][:, k, b * 128:(b + 1) * 128], rhs=Wo[:, k, o * 512:(o + 1) * 512],
                            start=(k == 0), stop=(k == NCH - 1)), [ot_b[osl], Wo_b], [pd_b[q]])
                    P.op("dve", lambda e, rs_=rs_, o=o, q=q: e.tensor_tensor(
                        out=xres[rs_][:, o * 512:(o + 1) * 512], in0=pd[q][:], in1=xres[rs_][:, o * 512:(o + 1) * 512],
                        op=ALU.add), [pd_b[q], xres_b[rs_]], [xres_b[rs_]])
                P.dma("act", dst[s, t0:t0 + 128, :], xres[rs_][:], [xres_b[rs_]], [], xres_sem[rs_])
        P.barrier()


WEIGHT_NAMES = ["norm_mix", "norm_ffn", "conv_w_in", "conv_b_in", "conv_dw", "conv_dw_b", "conv_ln_g", "conv_ln_b",
                "conv_w_out", "conv_b_out", "pool_w", "pool_b", "pool_scale", "fox_w_in", "fox_b_f", "fox_q_gain",
                "fox_k_gain", "fox_w_o", "ffn_w_up", "ffn_dw", "ffn_dw_b", "ffn_w_down"]


def build_program(n_seq, S, phases, wshapes):
    nc = bass.Bass("TRN2", target_bir_lowering=False)
    c = Ctx()
    c.n_seq, c.S = n_seq, S
    x = nc.dram_tensor("x", [n_seq, S, D], F32, kind="ExternalInput").ap()
    y = nc.dram_tensor("y", [n_seq, S, D], F32, kind="ExternalOutput").ap()
    c.w = {}
    for name in WEIGHT_NAMES:
        c.w[name] = nc.dram_tensor(name, list(wshapes[name]), F32, kind="ExternalInput").ap()
    if any(k == "fox" for k, _ in phases):
        c.qT_d = nc.dram_tensor("qT_d", [n_seq, D, S], BF16, kind="Internal").ap()
        c.kT_d = nc.dram_tensor("kT_d", [n_seq, D, S], BF16, kind="Internal").ap()
        c.v_d = nc.dram_tensor("v_d", [n_seq, S, D], BF16, kind="Internal").ap()
        c.o_d = nc.dram_tensor("o_d", [n_seq, D, S], BF16, kind="Internal").ap()
        c.cq_d = nc.dram_tensor("cq_d", [n_seq, NH, 3, S], BF16, kind="Internal").ap()
        c.ck_d = nc.dram_tensor("ck_d", [n_seq, NH, 3, S], BF16, kind="Internal").ap()
    with ExitStack() as st:
        P = Prog(nc, st)
        c.idf, c.idb, c.const_b = make_identity(P, st)
        c.nhalf = P.sbuf(st, "nhalf", [128, 1], F32)
        P.op("pool", lambda e: e.memset(c.nhalf[:], -0.5), [], [c.const_b], n=1)
        c.eps_col = P.sbuf(st, "eps_col", [128, 1], F32)
        P.op("pool", lambda e: e.memset(c.eps_col[:], EPS), [], [c.const_b], n=1)
        src = x
        for kind, li in phases:
            if kind == "ffn":
                ffn_phase(P, c, li, src, y)
            elif kind == "pool":
                pool_phase(P, c, li, src, y)
            elif kind == "conv":
                conv_phase(P, c, li, src, y)
            elif kind == "fox":
                fox_phase(P, c, li, src, y)
            else:
                raise ValueError(kind)
            src = y
        P.barrier()
        P.emit()
    return nc


ALL_PHASES = [("conv", 0), ("ffn", 0), ("pool", 1), ("ffn", 1), ("fox", 2), ("ffn", 2), ("conv", 3), ("ffn", 3)]
N_CORES = 8


def kernel(**inputs):
    x = np.ascontiguousarray(np.asarray(inputs["x"], dtype=np.float32))
    Bt, S, Dm = x.shape
    n_seq = Bt // N_CORES
    w = {k: np.ascontiguousarray(np.asarray(inputs[k], dtype=np.float32)) for k in WEIGHT_NAMES}
    nc = build_program(n_seq, S, ALL_PHASES, {k: v.shape for k, v in w.items()})
    in_maps = []
    for i in range(N_CORES):
        m = {"x": x[i * n_seq:(i + 1) * n_seq]}
        m.update(w)
        in_maps.append(m)
    res = run_bass_kernel_spmd(nc, in_maps, core_ids=list(range(N_CORES)))
    return np.concatenate([r["y"] for r in res.results], axis=0)
```
